# Optimizing a Trainium2 kernel written in Bass

```python
import math, functools
import jax, jax.numpy as jnp
from jax import lax
import numpy as np

D_MODEL = 1024
BATCH = 4
SEQ = 4096
DEPTH = 1
DEC_BATCH = 32
DEC_SEQ = 64
PAST_LEN = 1024

CHUNK = 64
D_MIX = D_MODEL
ATTN_WIDTH = D_MIX // 2
HEAD_DIM = 64
N_HEADS = ATTN_WIDTH // HEAD_DIM
N_KV_HEADS = 2
Q_PER_KV = N_HEADS // N_KV_HEADS
WINDOW = 128
WIN_CHUNKS = WINDOW // CHUNK
SSM_WIDTH = D_MIX - ATTN_WIDTH
SSM_GROUP = 16
SSM_GROUPS = SSM_WIDTH // SSM_GROUP
SSM_STATE = 64
D_FF = -(-8 * D_MODEL // (3 * 256)) * 256
PROJ_Q = N_HEADS * HEAD_DIM
PROJ_KV = N_KV_HEADS * HEAD_DIM
D_IN_PROJ = PROJ_Q + 2 * PROJ_KV + SSM_WIDTH
DN_ALPHA = (2.0 * DEPTH) ** 0.25
DN_BETA = (8.0 * DEPTH) ** -0.25
LN_EPS = 1e-5
NEG_INF = -1e30

kernel_name = "hymba_swa_sink_s5_deepnorm_stream_step"


def layer_norm(x, g, b):
    xf = x.astype(jnp.float32)
    mu = jnp.mean(xf, -1, keepdims=True)
    var = jnp.mean(jnp.square(xf - mu), -1, keepdims=True)
    return ((xf - mu) * lax.rsqrt(var + LN_EPS) * g.astype(jnp.float32) + b.astype(jnp.float32)).astype(x.dtype)


def sink_attention(q, k, v, sinks, mask):
    s = jnp.einsum('...qkgd,...skd->...kgqs', q, k).astype(jnp.float32) * (HEAD_DIM ** -0.5)
    if mask is not None:
        s = jnp.where(mask, s, NEG_INF)
    sink = sinks.astype(jnp.float32).reshape(N_KV_HEADS, Q_PER_KV, 1, 1)
    m = jnp.maximum(jnp.max(s, -1, keepdims=True), sink)
    p = jnp.exp(s - m)
    w = p / (jnp.sum(p, -1, keepdims=True) + jnp.exp(sink - m))
    return jnp.einsum('...kgqs,...skd->...qkgd', w.astype(v.dtype), v)


def prompt_window_attention(q, k, v, sinks, win_rows):
    B, L = q.shape[:2]
    nc = L // CHUNK
    qb = q.reshape(B, nc, CHUNK, N_KV_HEADS, Q_PER_KV, HEAD_DIM)
    pad = ((0, 0), (WIN_CHUNKS, 0), (0, 0), (0, 0), (0, 0))
    kp = jnp.pad(k.reshape(B, nc, CHUNK, N_KV_HEADS, HEAD_DIM), pad)
    vp = jnp.pad(v.reshape(B, nc, CHUNK, N_KV_HEADS, HEAD_DIM), pad)
    kband = jnp.concatenate([kp[:, j:j + nc] for j in range(WIN_CHUNKS + 1)], axis=2)
    vband = jnp.concatenate([vp[:, j:j + nc] for j in range(WIN_CHUNKS + 1)], axis=2)
    key_chunk = (jnp.arange(nc)[:, None] - WIN_CHUNKS
                 + jnp.repeat(jnp.arange(WIN_CHUNKS + 1), CHUNK)[None, :])
    mask = (key_chunk >= 0)[:, None, None, None, :]
    out = sink_attention(qb, kband, vband, sinks, mask)
    return out.reshape(B, L, PROJ_Q), k[:, -win_rows:], v[:, -win_rows:]


def sample_window_attention(q, k, v, sinks, cache_k, cache_v):
    B, S = q.shape[:2]
    k_all = jnp.concatenate([cache_k.astype(k.dtype), k], axis=1)
    v_all = jnp.concatenate([cache_v.astype(v.dtype), v], axis=1)
    out = sink_attention(q.reshape(B, S, N_KV_HEADS, Q_PER_KV, HEAD_DIM), k_all, v_all, sinks, None)
    win_rows = cache_k.shape[1]
    return out.reshape(B, S, PROJ_Q), k_all[:, -win_rows:], v_all[:, -win_rows:]


def s5_mixer(u, h0_re, h0_im, lam_re, lam_im, log_step, b_re, b_im, c_re, c_im, d_skip, w_glu, b_glu):
    Bn, L = u.shape[:2]
    f32 = jnp.float32
    dt = jnp.exp(log_step.astype(f32))[:, None]
    lr, li = lam_re.astype(f32), lam_im.astype(f32)
    mag = jnp.exp(lr * dt)
    ab_re, ab_im = mag * jnp.cos(li * dt), mag * jnp.sin(li * dt)
    nr, ni = ab_re - 1.0, ab_im
    den = lr * lr + li * li
    f_re, f_im = (nr * lr + ni * li) / den, (ni * lr - nr * li) / den
    br, bi = b_re.astype(f32), b_im.astype(f32)
    bb_re = f_re[..., None] * br - f_im[..., None] * bi
    bb_im = f_re[..., None] * bi + f_im[..., None] * br
    ug = u.astype(f32).reshape(Bn, L, SSM_GROUPS, SSM_GROUP)
    bu_re = jnp.einsum('blgc,gpc->blgp', ug, bb_re)
    bu_im = jnp.einsum('blgc,gpc->blgp', ug, bb_im)
    h0r, h0i = h0_re.astype(f32), h0_im.astype(f32)
    bu_re = bu_re.at[:, 0].add(ab_re * h0r - ab_im * h0i)
    bu_im = bu_im.at[:, 0].add(ab_re * h0i + ab_im * h0r)
    a_re = jnp.broadcast_to(ab_re, bu_re.shape)
    a_im = jnp.broadcast_to(ab_im, bu_im.shape)

    def combine(e1, e2):
        a1r, a1i, b1r, b1i = e1
        a2r, a2i, b2r, b2i = e2
        return (a2r * a1r - a2i * a1i, a2r * a1i + a2i * a1r,
                a2r * b1r - a2i * b1i + b2r, a2r * b1i + a2i * b1r + b2i)

    _, _, hr, hi = lax.associative_scan(combine, (a_re, a_im, bu_re, bu_im), axis=1)
    y = (jnp.einsum('blgp,gcp->blgc', hr, c_re.astype(f32))
         - jnp.einsum('blgp,gcp->blgc', hi, c_im.astype(f32)))
    y = y.reshape(Bn, L, SSM_WIDTH) + d_skip.astype(f32) * u.astype(f32)
    y = jax.nn.gelu(y, approximate=False)
    y = y * jax.nn.sigmoid(y @ w_glu.astype(f32) + b_glu.astype(f32))
    return y.astype(u.dtype), hr[:, -1], hi[:, -1]


def trunk_layer(x, attn, h0_re, h0_im, w_in, lam_re, lam_im, log_step, b_re, b_im, c_re, c_im,
                d_skip, w_glu, b_glu, w_out, ln1_g, ln1_b, w_gate_up, w_down, ln2_g, ln2_b):
    B, L, _ = x.shape
    proj = x @ w_in
    q, k, v, u = jnp.split(proj, [PROJ_Q, PROJ_Q + PROJ_KV, PROJ_Q + 2 * PROJ_KV], axis=-1)
    q = q.reshape(B, L, N_HEADS, HEAD_DIM)
    k = k.reshape(B, L, N_KV_HEADS, HEAD_DIM)
    v = v.reshape(B, L, N_KV_HEADS, HEAD_DIM)
    a, new_k, new_v = attn(q, k, v)
    s, new_re, new_im = s5_mixer(u, h0_re, h0_im, lam_re, lam_im, log_step, b_re, b_im,
                                 c_re, c_im, d_skip, w_glu, b_glu)
    mix = jnp.concatenate([a, s.astype(a.dtype)], axis=-1) @ w_out
    h = layer_norm(DN_ALPHA * x + mix, ln1_g, ln1_b)
    g, up = jnp.split(h @ w_gate_up, 2, axis=-1)
    f = (jax.nn.silu(g) * up) @ w_down
    y = layer_norm(DN_ALPHA * h + f, ln2_g, ln2_b)
    return y, new_k, new_v, new_re, new_im


def setup_inputs(seed: int = 0) -> dict:
    key = jax.random.key(seed)
    ks = jax.random.split(key, 32)
    nrm = lambda k, shape, scale: scale * jax.random.normal(k, shape, jnp.float32)
    win_rows = min(WINDOW, PAST_LEN)
    n_idx = jnp.arange(SSM_STATE, dtype=jnp.float32)
    return {
        'x_prompt': nrm(ks[0], (BATCH, SEQ, D_MODEL), 1.0),
        'x_sample': nrm(ks[1], (DEC_BATCH, DEC_SEQ, D_MODEL), 1.0),
        'cache_win_k': nrm(ks[2], (DEPTH, DEC_BATCH, win_rows, N_KV_HEADS, HEAD_DIM), 1.0),
        'cache_win_v': nrm(ks[3], (DEPTH, DEC_BATCH, win_rows, N_KV_HEADS, HEAD_DIM), 1.0),
        'state_ssm_re': nrm(ks[4], (DEPTH, DEC_BATCH, SSM_GROUPS, SSM_STATE), 0.3),
        'state_ssm_im': nrm(ks[5], (DEPTH, DEC_BATCH, SSM_GROUPS, SSM_STATE), 0.3),
        'ln_in_g': 1.0 + nrm(ks[6], (D_MODEL,), 0.02),
        'ln_in_b': nrm(ks[7], (D_MODEL,), 0.02),
        'w_in': nrm(ks[8], (DEPTH, D_MODEL, D_IN_PROJ), D_MODEL ** -0.5),
        'attn_sinks': nrm(ks[9], (DEPTH, N_HEADS), 0.5),
        'ssm_lambda_re': -0.5 + nrm(ks[10], (DEPTH, SSM_GROUPS, SSM_STATE), 0.01),
        'ssm_lambda_im': math.pi * n_idx + nrm(ks[11], (DEPTH, SSM_GROUPS, SSM_STATE), 0.01),
        'ssm_log_step': jax.random.uniform(ks[12], (DEPTH, SSM_GROUPS), jnp.float32,
                                           math.log(1e-3), math.log(1e-1)),
        'ssm_b_re': nrm(ks[13], (DEPTH, SSM_GROUPS, SSM_STATE, SSM_GROUP), (2 * SSM_GROUP) ** -0.5),
        'ssm_b_im': nrm(ks[14], (DEPTH, SSM_GROUPS, SSM_STATE, SSM_GROUP), (2 * SSM_GROUP) ** -0.5),
        'ssm_c_re': nrm(ks[15], (DEPTH, SSM_GROUPS, SSM_GROUP, SSM_STATE), (2 * SSM_STATE) ** -0.5),
        'ssm_c_im': nrm(ks[16], (DEPTH, SSM_GROUPS, SSM_GROUP, SSM_STATE), (2 * SSM_STATE) ** -0.5),
        'ssm_d': nrm(ks[17], (DEPTH, SSM_WIDTH), 1.0),
        'w_glu': nrm(ks[18], (DEPTH, SSM_WIDTH, SSM_WIDTH), SSM_WIDTH ** -0.5),
        'b_glu': nrm(ks[19], (DEPTH, SSM_WIDTH), 0.02),
        'w_out': nrm(ks[20], (DEPTH, D_MIX, D_MODEL), DN_BETA * D_MIX ** -0.5),
        'ln1_g': 1.0 + nrm(ks[21], (DEPTH, D_MODEL), 0.02),
        'ln1_b': nrm(ks[22], (DEPTH, D_MODEL), 0.02),
        'w_gate_up': nrm(ks[23], (DEPTH, D_MODEL, 2 * D_FF), D_MODEL ** -0.5),
        'w_down': nrm(ks[24], (DEPTH, D_FF, D_MODEL), DN_BETA * D_FF ** -0.5),
        'ln2_g': 1.0 + nrm(ks[25], (DEPTH, D_MODEL), 0.02),
        'ln2_b': nrm(ks[26], (DEPTH, D_MODEL), 0.02),
    }


def reference(x_prompt, x_sample, cache_win_k, cache_win_v, state_ssm_re, state_ssm_im,
              ln_in_g, ln_in_b, w_in, attn_sinks, ssm_lambda_re, ssm_lambda_im, ssm_log_step,
              ssm_b_re, ssm_b_im, ssm_c_re, ssm_c_im, ssm_d, w_glu, b_glu, w_out,
              ln1_g, ln1_b, w_gate_up, w_down, ln2_g, ln2_b):
    win_rows = cache_win_k.shape[2]
    xp = layer_norm(x_prompt, ln_in_g, ln_in_b)
    xs = layer_norm(x_sample, ln_in_g, ln_in_b)
    kp_l, vp_l, rp_l, ip_l, ks_l, vs_l, rs_l, is_l = [], [], [], [], [], [], [], []
    for l in range(DEPTH):
        shared = (w_in[l], ssm_lambda_re[l], ssm_lambda_im[l], ssm_log_step[l], ssm_b_re[l], ssm_b_im[l],
                  ssm_c_re[l], ssm_c_im[l], ssm_d[l], w_glu[l], b_glu[l], w_out[l],
                  ln1_g[l], ln1_b[l], w_gate_up[l], w_down[l], ln2_g[l], ln2_b[l])
        attn_p = functools.partial(prompt_window_attention, sinks=attn_sinks[l], win_rows=win_rows)
        h0 = jnp.zeros((xp.shape[0], SSM_GROUPS, SSM_STATE), jnp.float32)
        xp, k1, v1, r1, i1 = trunk_layer(xp, attn_p, h0, h0, *shared)
        attn_s = functools.partial(sample_window_attention, sinks=attn_sinks[l],
                                   cache_k=cache_win_k[l], cache_v=cache_win_v[l])
        xs, k2, v2, r2, i2 = trunk_layer(xs, attn_s, state_ssm_re[l], state_ssm_im[l], *shared)
        kp_l.append(k1); vp_l.append(v1); rp_l.append(r1); ip_l.append(i1)
        ks_l.append(k2); vs_l.append(v2); rs_l.append(r2); is_l.append(i2)
    return (xp, xs,
            jnp.stack(kp_l), jnp.stack(vp_l), jnp.stack(rp_l), jnp.stack(ip_l),
            jnp.stack(ks_l), jnp.stack(vs_l), jnp.stack(rs_l), jnp.stack(is_l))
```

```python
import math
from contextlib import ExitStack
import numpy as np
import concourse.bass as bass
import concourse.mybir as mybir
from concourse.bass_utils import run_bass_kernel_spmd

F32 = mybir.dt.float32
BF16 = mybir.dt.bfloat16
I32 = mybir.dt.int32
AF = mybir.ActivationFunctionType
ALU = mybir.AluOpType

D = 1024
KT = 8
NST = 256
T = 8
NCH = NST // T
DFF = 2816
NJ = 22
DIN = 1280
ALPHA = 2.0 ** 0.25
EPS = 1e-5
NMAIN = 2048
NSAMP = 256
NTOK = NMAIN + NSAMP
PI = math.pi
KLIST = list(range(9)) + [8 * c for c in range(NCH)] + list(range(7, -1, -1)) + [T * NCH]
NK = len(KLIST)
WARM = True
TOP_STACK = [None]
DETACHED = {}
COMPUTE = ("pe", "act", "dve", "pool")
QUEUES = ("sp",)


class Prog:
    ISSUE = {"sp": 60.0, "pool": 900.0, "act": 150.0}

    def __init__(self, nc, tag, pers):
        self.nc = nc
        self.tag = tag
        self.pers = pers
        self.es = ExitStack()
        self.all = []
        self.detached_names = set()
        self.prio_bias = 0
        self.pe_ghz = 1.25
        self.filler = None
        self.fill_dur = 230.0
        self.fill_gap = 900.0
        self.pre_waits = []
        self.lastw = {}
        self.readers = {}
        self.sems = {}

    def sb(self, name, shape, dt):
        return self.es.enter_context(self.nc.sbuf_tensor(self.tag + name, list(shape), dt))

    def ps(self, name, shape, dt=F32):
        return self.es.enter_context(self.nc.psum_tensor(self.tag + name, list(shape), dt))

    def _record(self, eng, fn, reads, writes, dur, dma, lat):
        deps = set()
        for k in reads:
            ev = self.lastw.get(k)
            if ev is not None:
                deps.add(ev)
        for k in writes:
            ev = self.lastw.get(k)
            if ev is not None:
                deps.add(ev)
            deps.update(self.readers.get(k, ()))
        oid = len(self.all)
        self.all.append(dict(id=oid, eng=eng, fn=fn, deps=deps, dur=dur, dma=dma, lat=lat, prio=oid + self.prio_bias))
        for k in reads:
            self.readers.setdefault(k, []).append(oid)
        for k in writes:
            self.lastw[k] = oid
            self.readers[k] = []
        return oid

    def op(self, eng, fn, reads=(), writes=(), dur=100.0):
        return self._record(eng, fn, reads, writes, dur, None, 0.0)

    def dma(self, q, sem, fn, reads=(), writes=(), final=False, nbytes=65536, detached=False):
        if detached:
            self.detached_names.add(sem)
        return self._record(q, fn, reads, writes, self.ISSUE.get(q, 100.0), sem, 2000.0 + nbytes / 120.0)

    def _sem(self, key):
        if key not in self.sems:
            name = self.tag + "s_" + "_".join(str(x) for x in key)
            stack = TOP_STACK[0] if (key[0] == 'd' and key[1] in self.detached_names) else self.pers
            self.sems[key] = stack.enter_context(self.nc.semaphore(name))
        return self.sems[key]

    def _schedule(self):
        import heapq
        ops = self.all
        n = len(ops)
        succ = [[] for _ in range(n)]
        indeg = [0] * n
        for o in ops:
            indeg[o["id"]] = len(o["deps"])
            for d in o["deps"]:
                succ[d].append(o["id"])
        engs = COMPUTE + QUEUES
        ready = {e: [] for e in engs}
        ready_t = [0.0] * n
        done_t = [0.0] * n
        free_at = {e: 0.0 for e in engs}
        order = {e: [] for e in engs}
        for o in ops:
            if indeg[o["id"]] == 0:
                heapq.heappush(ready[o["eng"]], o["id"])
        remaining = n
        while remaining:
            best = None
            for e in engs:
                if not ready[e]:
                    continue
                t_free = free_at[e]
                cand = [i for i in ready[e] if ready_t[i] <= t_free]
                if cand:
                    i = min(cand, key=lambda j: (ops[j]["prio"], j))
                    t = t_free
                else:
                    i = min(ready[e], key=lambda j: (ready_t[j], ops[j]["prio"], j))
                    t = ready_t[i]
                if best is None or t < best[0]:
                    best = (t, e, i)
            t, e, i = best
            if e == "pe" and self.filler is not None and t - free_at[e] > self.fill_gap:
                nf = int((t - free_at[e] - 200.0) // self.fill_dur)
                order[e].extend([-1] * max(0, nf))
            ready[e].remove(i)
            heapq.heapify(ready[e])
            o = ops[i]
            order[e].append(i)
            end = t + o["dur"]
            free_at[e] = end
            done_t[i] = end + o["lat"]
            remaining -= 1
            for j in succ[i]:
                indeg[j] -= 1
                ready_t[j] = max(ready_t[j], done_t[i])
                if indeg[j] == 0:
                    heapq.heappush(ready[ops[j]["eng"]], j)
        self.est_ns = max(done_t) if n else 0.0
        return order

    def emit(self):
        nc = self.nc
        ops = self.all
        order = self._schedule()
        pos = {}
        for e, lst in order.items():
            for p, i in enumerate(lst):
                if i >= 0:
                    pos[i] = p
        dma_val = {}
        dma_cnt = {}
        for e in COMPUTE + QUEUES:
            for i in order[e]:
                if i < 0:
                    continue
                sname = ops[i]["dma"]
                if sname is not None:
                    dma_cnt[sname] = dma_cnt.get(sname, 0) + 1
                    dma_val[i] = 16 * dma_cnt[sname]
        waits = {}
        milestone = {e: set() for e in COMPUTE}
        for e in COMPUTE + QUEUES:
            known = {}
            for i in order[e]:
                if i < 0:
                    continue
                w = {}
                for d in ops[i]["deps"]:
                    od = ops[d]
                    if od["dma"] is not None:
                        key, val = ('d', od["dma"]), dma_val[d]
                    else:
                        if od["eng"] == e and e == "pe":
                            continue
                        key, val = ('c', od["eng"]), pos[d]
                    if known.get(key, -1) >= val:
                        continue
                    if w.get(key, -1) < val:
                        w[key] = val
                for key, val in w.items():
                    known[key] = val
                    if key[0] == 'c':
                        milestone[key[1]].add(val)
                waits[i] = list(w.items())
        mval = {}
        for e in COMPUTE:
            ms = sorted(milestone[e])
            mval[e] = {p: k + 1 for k, p in enumerate(ms)}
            if ms:
                self._sem(('c', e))
        for sname in dma_cnt:
            self._sem(('d', sname))

        def run_engine(e, eng):
            if e == "sp":
                for (h, v) in self.pre_waits:
                    eng.wait_ge(h, v)
            for p, i in enumerate(order[e]):
                if i < 0:
                    self.filler(eng)
                    continue
                o = ops[i]
                for (key, val) in waits[i]:
                    if key[0] == 'c':
                        eng.wait_ge(self._sem(key), mval[key[1]][val])
                    else:
                        eng.wait_ge(self._sem(key), val)
                ins = o["fn"](eng)
                if o["dma"] is not None:
                    ins.then_inc(self._sem(('d', o["dma"])), 16)
                elif e in COMPUTE and p in mval[e]:
                    ins.then_inc(self._sem(('c', e)), 1)
            if e == "sp":
                for a, cnt in dma_cnt.items():
                    if a in self.detached_names:
                        DETACHED[a] = (self._sem(('d', a)), 16 * cnt)
                    else:
                        eng.wait_ge(self._sem(('d', a)), 16 * cnt)

        with nc.Block() as block:
            @block.sync
            def _(eng):
                run_engine("sp", eng)

            @block.tensor
            def _(eng):
                run_engine("pe", eng)

            @block.scalar
            def _(eng):
                run_engine("act", eng)

            @block.vector
            def _(eng):
                run_engine("dve", eng)

            @block.gpsimd
            def _(eng):
                run_engine("pool", eng)
        self.es.close()

    @staticmethod
    def _fs(ap):
        n = 1
        for d in ap.shape[1:]:
            n *= d
        return n

    def mm(self, out, lhsT, rhs, start, stop, reads, writes, tp=None):
        if getattr(writes, "grp", None) is not None:
            writes = list(writes) + [writes.grp]
        dur = max(64, self._fs(rhs)) / self.pe_ghz + 12.0
        if tp is None:
            return self.op("pe", lambda e: e.matmul(out, lhsT=lhsT, rhs=rhs, start=start, stop=stop), reads, writes, dur)
        return self.op("pe", lambda e: e.matmul(out, lhsT=lhsT, rhs=rhs, start=start, stop=stop,
                                                tile_position=tp, skip_group_check=True), reads, writes, dur)

    def tr(self, out, in_, ident, reads, writes):
        return self.op("pe", lambda e: e.transpose(out=out, in_=in_, identity=ident), reads, writes, 110.0)

    def act(self, out, in_, func, reads, writes, scale=1.0, bias=0.0):
        extra = 0.0 if func in (AF.Identity, AF.Copy) else 500.0
        return self.op("act", lambda e: e.activation(out=out, in_=in_, func=func, bias=bias, scale=scale),
                       reads, writes, 230.0 + extra + self._fs(out) / 1.2)

    def _edur(self, eng, out):
        n = self._fs(out)
        if eng == "dve":
            return 120.0 + n * (2.6 if n >= 256 else 1.1)
        if eng == "pool":
            return 250.0 + n * 2.4
        return 230.0 + n / 1.2

    def tt(self, eng, out, in0, in1, op, reads, writes):
        return self.op(eng, lambda e: e.tensor_tensor(out=out, in0=in0, in1=in1, op=op), reads, writes,
                       self._edur(eng, out))

    def ts(self, eng, out, in0, s1, s2, op0, op1, reads, writes):
        if op1 is None:
            return self.op(eng, lambda e: e.tensor_scalar(out=out, in0=in0, scalar1=s1, scalar2=None, op0=op0),
                           reads, writes, self._edur(eng, out))
        return self.op(eng, lambda e: e.tensor_scalar(out=out, in0=in0, scalar1=s1, scalar2=s2, op0=op0, op1=op1),
                       reads, writes, self._edur(eng, out))

    def stt(self, out, in0, scalar, in1, op0, op1, reads, writes):
        return self.op("dve", lambda e: e.scalar_tensor_tensor(out=out, in0=in0, scalar=scalar, in1=in1,
                                                               op0=op0, op1=op1), reads, writes,
                       120.0 + 2.6 * self._fs(out))

    def cp(self, eng, out, in_, reads, writes):
        if eng == "act":
            return self.op("act", lambda e: e.copy(out=out, in_=in_), reads, writes, self._edur("act", out))
        return self.op(eng, lambda e: e.tensor_copy(out=out, in_=in_), reads, writes, self._edur(eng, out))


class _Stop(Exception):
    pass


def build_nc(stop=None):
    nc = bass.Bass("TRN2", target_bir_lowering=False)
    TOP_STACK[0] = ExitStack()
    DETACHED.clear()
    try:
        _build(nc, stop)
    except _Stop:
        pass
    TOP_STACK[0].close()
    return nc


def _build(nc, stop):

    def din(name, shape, dt=F32):
        return nc.dram_tensor(name, list(shape), dt, kind="ExternalInput").ap()

    def dout(name, shape, dt=F32):
        return nc.dram_tensor(name, list(shape), dt, kind="ExternalOutput").ap()

    xprev = din("xprev", [NMAIN, D])
    xmain = din("xmain", [NMAIN, D])
    xsamp = din("xsamp", [NSAMP, D])
    flag_d = din("flag", [128, 1])
    hbias_d = din("hbias", [128, 4])
    ident_d = din("ident", [128, 128])
    emask_d = din("emask", [128, 2])
    gbc_d = din("gbc", [6, 128, D])
    gcol_d = din("gcol", [128, 6, KT])
    kvec_d = din("kvec", [128, NK])
    lamr_d = din("lamr", [128, 16])
    lami_d = din("lami", [128, 16])
    lstep_d = din("lstep", [128, 16])
    bqr_d = din("bqr", [128, 16, 16])
    bqi_d = din("bqi", [128, 16, 16])
    cqr_d = din("cqr", [128, 16, 16])
    cqi_d = din("cqi", [128, 16, 16])
    dcol_d = din("dcol", [128, 4])
    bglu_d = din("bglucol", [128, 4])
    sink_d = din("sinkbc", [128, 8])
    ck_d = din("ck", [4, 128, 128])
    cv_d = din("cv", [4, 128, 128])
    str_d = din("st_re", [128, 16, 4])
    sti_d = din("st_im", [128, 16, 4])
    w_in_d = din("w_in", [D, DIN])
    w_glu_d = din("w_glu", [512, 512])
    w_out_d = din("w_out", [D, D])
    w_gu_d = din("w_gate_up", [D, 2 * DFF])
    w_dn_d = din("w_down", [DFF, D])

    y_main = dout("y_main", [NMAIN, D])
    y_samp = dout("y_samp", [NSAMP, D])
    kwin_p = dout("kwin_p", [128, 128])
    vwin_p = dout("vwin_p", [128, 128])
    sre_p = dout("sre_p", [128, 16])
    sim_p = dout("sim_p", [128, 16])
    kwin_s = dout("kwin_s", [4, 128, 128])
    vwin_s = dout("vwin_s", [4, 128, 128])
    sre_s = dout("sre_s", [128, 16, 4])
    sim_s = dout("sim_s", [128, 16, 4])

    wgu_nat = nc.dram_tensor("wgu_nat", [D, 2 * DFF], BF16).ap()
    wdn_sc = nc.dram_tensor("wdn_sc", [DFF, D], BF16).ap()
    hT_sc = nc.dram_tensor("hT_sc", [128, KT, NTOK], BF16).ap()
    r1_sc = nc.dram_tensor("r1_sc", [NTOK, D], F32).ap()

    ES = ExitStack()

    def pers(name, shape, dt):
        return ES.enter_context(nc.sbuf_tensor("p_" + name, list(shape), dt))

    identb = pers("identb", [128, 128], BF16)
    emask = pers("emask", [128, 2], F32)
    flag = pers("flag", [128, 1], F32)
    hbias = pers("hbias", [128, 4], F32)
    zbias = pers("zbias", [128, 1], F32)
    mhalf = pers("mhalf", [128, 1], F32)
    cfm = pers("cfm", [128, 10], F32)
    ckd = pers("ckd", [128, 4], F32)
    ckvb = pers("ckvb", [128, 256], F32)
    gcol = pers("gcol", [128, 6, KT], F32)
    G0a = pers("G0a", [128, D], F32)
    B0a = pers("B0a", [128, D], F32)
    dcol = pers("dcol", [128, 4], F32)
    bglu = pers("bglu", [128, 4], F32)
    sinkexp = pers("sinkexp", [128, 8], F32)
    kvec = pers("kvec", [128, NK], F32)
    w_inb = pers("w_inb", [128, KT, DIN], BF16)
    w_kd = pers("w_kd", [128, KT, 128], BF16)
    w_outb = pers("w_outb", [128, KT, D], BF16)
    w_glub = pers("w_glub", [128, 4, 512], BF16)
    cosMt = pers("cosMt", [128, 16, NCH], F32)
    sinMt = pers("sinMt", [128, 16, NCH], F32)
    aLr_t = pers("aLr_t", [128, 16], F32)
    aLi_t = pers("aLi_t", [128, 16], F32)
    PcR_t = pers("PcR_t", [128, 16, NCH], F32)
    PcI_t = pers("PcI_t", [128, 16, NCH], F32)
    PvR_t = pers("PvR_t", [128, 16, NCH], F32)
    PvI_t = pers("PvI_t", [128, 16, NCH], F32)
    Wst = pers("Wst", [128, 4, 8, 2, 128], BF16)
    Cmod = pers("Cmod", [128, 16, 8, 2, 32], BF16)
    Kblk = pers("Kblk", [128, 4, 8, 128], BF16)
    rhot = pers("rhot", [128, 16, NCH], F32)
    st = dict(tp=0, gb=0)
    P = Prog(nc, "a_", ES)
    tpb = [P.ps("tp%d" % i, [128, 1024], BF16) for i in range(2)]
    gb = [P.ps("gb%d" % i, [128, 512], F32) for i in range(6)]

    class KeyList(list):
        grp = None

    def _kl(keys, grp=None):
        k = KeyList(keys)
        k.grp = grp
        return k

    def next_tp(kind="F"):
        if kind == "F":
            return tpb[0], _kl([("tp", 0)])
        if kind == "L":
            return tpb[1], _kl([("tp", 1, q) for q in range(4)])
        q = st["tp"] % 4
        st["tp"] += 1
        return tpb[1][:, q * 256:(q + 1) * 256], _kl([("tp", 1, q)])

    def next_gb(pool="M"):
        grp = {"F": "F", "S": "S", "Sh": "S", "Q": "M", "M": "M"}[pool]
        c = st.setdefault(grp, 0)
        st[grp] = c + 1
        b = {"F": 0, "S": 2, "M": 4}[grp] + c % 2
        if grp == "F" and st.get("warm"):
            b = 0
        return gb[b], _kl([("gb", b)])

    identf = P.sb("identf", [128, 128], F32)

    ldc = dict(n=0)

    def ld(sem, dst, src, key):
        ldc["n"] += 1
        return P.dma("sp", "u%d" % ldc["n"], lambda e: e.dma_start(out=dst, in_=src), writes=[key])

    ld("c0", identf[:], ident_d, "identf")
    ld("c0", emask[:], emask_d, "emask")
    ld("c0", flag[:], flag_d, "flag")
    ld("c0", hbias[:], hbias_d, "hbias")
    ld("c0", gcol[:], gcol_d, "gcol")
    ld("c0", dcol[:], dcol_d, "dcol")
    ld("c0", bglu[:], bglu_d, "bglu")
    P.op("act", lambda e: e.mul(out=bglu[:], in_=bglu[:], mul=0.5), ["bglu"], ["bglu"])
    ld("c0", sinkexp[:], sink_d, "sinkexp")
    ld("c0", kvec[:], kvec_d, "kvec")
    P.cp("dve", identb[:], identf[:], ["identf"], ["identb"])
    P.op("pool", lambda e: e.memset(zbias[:], 0.0), writes=["zbias"])
    P.op("pool", lambda e: e.memset(mhalf[:], -0.5), writes=["mhalf"])
    P.act(sinkexp[:], sinkexp[:], AF.Exp, ["sinkexp"], ["sinkexp"])
    for i, tbl in enumerate((G0a, B0a)):
        ld("c1", tbl[:], gbc_d[i], ("tbl", i))
        P.op("act", lambda e, tbl=tbl: e.mul(out=tbl[:], in_=tbl[:], mul=ALPHA), [("tbl", i)], [("tbl", i)])

    stg = [P.sb("stg%d" % i, [128, DIN], F32) for i in range(3)]
    cast_engs = ["dve", "pool", "act"]
    sc = dict(n=0)

    def stage_cast(src_ap, ncols, dst_ap, mul=None):
        i = sc["n"] % 3
        eng = cast_engs[sc["n"] % 3]
        sc["n"] += 1
        P.dma("sp", "stg%d" % i, lambda e: e.dma_start(out=stg[i][:, 0:ncols], in_=src_ap), writes=[("stg", i)],
              nbytes=128 * ncols * 4)
        if mul is None:
            P.cp(eng, dst_ap, stg[i][:, 0:ncols], [("stg", i)], ["wres"])
        else:
            P.op("act", lambda e: e.mul(out=dst_ap, in_=stg[i][:, 0:ncols], mul=mul), [("stg", i)], ["wres"],
                 230.0 + ncols / 1.2)

    onesr = P.sb("onesr", [1, 128], F32)
    one1 = P.sb("one1", [1, 1], F32)
    crow = P.sb("crow", [1, DIN], F32)
    crkd = P.sb("crkd", [1, 128], F32)
    P.op("pool", lambda e: e.memset(onesr[:], 1.0), writes=["onesr"])
    P.op("pool", lambda e: e.memset(one1[:], 1.0), writes=["one1"])
    P.op("pool", lambda e: e.memset(crkd[:], 0.0), writes=["crkd"])
    cbanks = [next_gb("F"), next_gb("S"), next_gb("M")]
    for kt in range(KT):
        i = sc["n"] % 3
        sc["n"] += 1
        P.dma("sp", "stg%d" % i, lambda e, i=i, kt=kt: e.dma_start(out=stg[i][:, 0:DIN], in_=w_in_d[kt * 128:(kt + 1) * 128, :]),
              writes=[("stg", i)], nbytes=128 * DIN * 4)
        P.act(w_inb[:, kt, :], stg[i][:, 0:DIN], AF.Identity, [("stg", i), "gcol"], ["wres"], scale=gcol[:, 0, kt:kt + 1])
        for cb, (c0, cn) in enumerate(((0, 512), (512, 512), (1024, 256))):
            bank, bk = cbanks[cb]
            P.mm(bank[0:1, 0:cn], gcol[:, 1, kt:kt + 1], stg[i][:, c0:c0 + cn], kt == 0, kt == KT - 1,
                 [("stg", i), "gcol"], bk)
    for cb, (c0, cn) in enumerate(((0, 512), (512, 512), (1024, 256))):
        bank, bk = cbanks[cb]
        P.cp("dve", crow[:, c0:c0 + cn], bank[0:1, 0:cn], bk, ["crow"])
    P.cp("dve", crkd[:, 0:64], crow[:, 576:640], ["crow", "crkd"], ["crkd"])
    P.cp("dve", crkd[:, 64:128], crow[:, 512:576], ["crow", "crkd"], ["crkd"])
    bank, bk = next_gb("F")
    for ot in range(10):
        P.mm(bank[:, ot:ot + 1], crow[:, ot * 128:(ot + 1) * 128], one1[:], True, True, ["crow", "one1"], bk)
    P.mm(bank[:, 16:17], crkd[:], one1[:], True, True, ["crkd", "one1"], bk)
    P.cp("dve", cfm[:], bank[:, 0:10], bk, ["cfm"])
    P.cp("dve", ckd[:, 0:1], bank[:, 16:17], bk, ["ckd"])
    bank, bk = next_gb("S")
    P.mm(bank[:, 0:256], onesr[:], crow[:, 512:768], True, True, ["crow", "onesr"], bk)
    P.cp("dve", ckvb[:], bank[:, 0:256], bk, ["ckvb"])
    for kt in range(KT):
        stage_cast(w_out_d[kt * 128:(kt + 1) * 128, :], D, w_outb[:, kt, :], 0.5 if kt >= 4 else None)
    for kt in range(4):
        stage_cast(w_glu_d[kt * 128:(kt + 1) * 128, :], 512, w_glub[:, kt, :])
    P.cp("pool", w_kd[:, :, 0:64], w_inb[:, :, 576:640], ["wres"], ["wres2"])
    P.cp("pool", w_kd[:, :, 64:128], w_inb[:, :, 512:576], ["wres"], ["wres2"])
    def stop_here(name):
        if stop == name:
            P.emit()
            raise _Stop()

    def new_prog(tag):
        stop_here({"c_": "b"}[tag])
        P.emit()
        nc.all_engine_barrier()
        st["tp"] = 0
        st["gb"] = 0
        return Prog(nc, tag, ES)

    lamr = P.sb("lamr", [128, 16], F32)
    lami = P.sb("lami", [128, 16], F32)
    dtt = P.sb("dtt", [128, 16], F32)
    lam = P.sb("lam", [128, 16], F32)
    th = P.sb("th", [128, 16], F32)
    ang = P.sb("ang", [128, 16, NK], F32)
    angc = P.sb("angc", [128, 16, NK], F32)
    tmpf = P.sb("tmpf", [128, 16, NK], F32)
    tmpi = P.sb("tmpi", [128, 16, NK], I32)
    msk = P.sb("msk", [128, 16, NK], F32)
    mag = P.sb("mag", [128, 16, NK], F32)
    ld("c2", lamr[:], lamr_d, "lamr")
    ld("c2", lami[:], lami_d, "lami")
    ld("c2", dtt[:], lstep_d, "dtt")
    P.act(dtt[:], dtt[:], AF.Exp, ["dtt"], ["dtt"])
    P.tt("dve", lam[:], lamr[:], dtt[:], ALU.mult, ["lamr", "dtt"], ["lam"])
    P.tt("dve", th[:], lami[:], dtt[:], ALU.mult, ["lami", "dtt"], ["th"])
    kv_b = kvec[:].unsqueeze(1).broadcast_to([128, 16, NK])
    P.tt("dve", ang[:], th[:].unsqueeze(2).broadcast_to([128, 16, NK]), kv_b, ALU.mult, ["th", "kvec"], ["ang"])
    P.tt("dve", mag[:], lam[:].unsqueeze(2).broadcast_to([128, 16, NK]), kv_b, ALU.mult, ["lam", "kvec"], ["mag"])
    P.act(mag[:], mag[:], AF.Exp, ["mag"], ["mag"])
    P.ts("dve", angc[:], ang[:], PI / 2, None, ALU.add, None, ["ang"], ["angc"])

    def sin_reduced(dst, src, key_src, key_dst):
        P.ts("dve", tmpf[:], src[:], 1.0 / (2 * PI), None, ALU.mult, None, [key_src], ["tmpf"])
        P.cp("dve", tmpi[:], tmpf[:], ["tmpf"], ["tmpi"])
        P.cp("dve", tmpf[:], tmpi[:], ["tmpi"], ["tmpf"])
        P.stt(src[:], tmpf[:], -2 * PI, src[:], ALU.mult, ALU.add, ["tmpf", key_src], [key_src])
        P.ts("dve", msk[:], src[:], PI, None, ALU.is_gt, None, [key_src], ["msk"])
        P.stt(src[:], msk[:], -2 * PI, src[:], ALU.mult, ALU.add, ["msk", key_src], [key_src])
        P.ts("dve", msk[:], src[:], -PI, None, ALU.is_lt, None, [key_src], ["msk"])
        P.stt(src[:], msk[:], 2 * PI, src[:], ALU.mult, ALU.add, ["msk", key_src], [key_src])
        P.ts("dve", src[:], src[:], 3.1415925, -3.1415925, ALU.min, ALU.max, [key_src], [key_src])
        P.act(dst[:], src[:], AF.Sin, [key_src], [key_dst])

    sin_reduced(ang, ang, "ang", "ang")
    sin_reduced(angc, angc, "angc", "angc")
    P.cp("dve", cosMt[:], angc[:, :, 9:9 + NCH], ["angc"], ["cosMt"])
    P.cp("dve", sinMt[:], ang[:, :, 9:9 + NCH], ["ang"], ["sinMt"])
    P.tt("dve", angc[:], mag[:], angc[:], ALU.mult, ["mag", "angc"], ["angc"])
    P.tt("dve", ang[:], mag[:], ang[:], ALU.mult, ["mag", "ang"], ["ang"])

    s16 = [P.sb("s16_%d" % i, [128, 16], F32) for i in range(8)]
    nr, den, fre, fim, t0, t1 = s16[0], s16[1], s16[2], s16[3], s16[4], s16[5]
    ar = angc[:, :, 1]
    ai = ang[:, :, 1]
    P.ts("dve", nr[:], ar, -1.0, None, ALU.add, None, ["angc"], ["nr"])
    P.tt("dve", den[:], lamr[:], lamr[:], ALU.mult, ["lamr"], ["den"])
    P.tt("dve", t0[:], lami[:], lami[:], ALU.mult, ["lami"], ["t0"])
    P.tt("dve", den[:], den[:], t0[:], ALU.add, ["den", "t0"], ["den"])
    P.op("dve", lambda e: e.reciprocal(out=den[:], in_=den[:]), ["den"], ["den"])
    P.tt("dve", t0[:], nr[:], lamr[:], ALU.mult, ["nr", "lamr"], ["t0"])
    P.tt("dve", t1[:], ai, lami[:], ALU.mult, ["ang", "lami"], ["t1"])
    P.tt("dve", t0[:], t0[:], t1[:], ALU.add, ["t0", "t1"], ["t0"])
    P.tt("dve", fre[:], t0[:], den[:], ALU.mult, ["t0", "den"], ["fre"])
    P.tt("dve", t0[:], ai, lamr[:], ALU.mult, ["ang", "lamr"], ["t0"])
    P.tt("dve", t1[:], nr[:], lami[:], ALU.mult, ["nr", "lami"], ["t1"])
    P.tt("dve", t0[:], t0[:], t1[:], ALU.subtract, ["t0", "t1"], ["t0"])
    P.tt("dve", fim[:], t0[:], den[:], ALU.mult, ["t0", "den"], ["fim"])

    Bqr = P.sb("Bqr", [128, 16, 16], F32)
    Bqi = P.sb("Bqi", [128, 16, 16], F32)
    Cqr = P.sb("Cqr", [128, 16, 16], F32)
    Cqi = P.sb("Cqi", [128, 16, 16], F32)
    Bbr = P.sb("Bbr", [128, 16, 16], F32)
    Bbi = P.sb("Bbi", [128, 16, 16], F32)
    u0 = P.sb("u0", [128, 16, 16], F32)
    u1 = P.sb("u1", [128, 16, 16], F32)
    ld("c2", Bqr[:], bqr_d, "Bqr")
    ld("c2", Bqi[:], bqi_d, "Bqi")
    ld("c2", Cqr[:], cqr_d, "Cqr")
    ld("c2", Cqi[:], cqi_d, "Cqi")
    fre_b = fre[:].unsqueeze(2).broadcast_to([128, 16, 16])
    fim_b = fim[:].unsqueeze(2).broadcast_to([128, 16, 16])
    P.tt("dve", u0[:], Bqr[:], fre_b, ALU.mult, ["Bqr", "fre"], ["u0"])
    P.tt("dve", u1[:], Bqi[:], fim_b, ALU.mult, ["Bqi", "fim"], ["u1"])
    P.tt("dve", Bbr[:], u0[:], u1[:], ALU.subtract, ["u0", "u1"], ["Bbr"])
    P.tt("dve", u0[:], Bqi[:], fre_b, ALU.mult, ["Bqi", "fre"], ["u0"])
    P.tt("dve", u1[:], Bqr[:], fim_b, ALU.mult, ["Bqr", "fim"], ["u1"])
    P.tt("dve", Bbi[:], u0[:], u1[:], ALU.add, ["u0", "u1"], ["Bbi"])

    def cprod(dst_r, dst_i, xr, xi, pidx0, n, kr, ki):
        pr = angc[:, :, pidx0:pidx0 + n].unsqueeze(3).broadcast_to([128, 16, n, 16])
        pi = ang[:, :, pidx0:pidx0 + n].unsqueeze(3).broadcast_to([128, 16, n, 16])
        xrb = xr[:].unsqueeze(2).broadcast_to([128, 16, n, 16])
        xib = xi[:].unsqueeze(2).broadcast_to([128, 16, n, 16])
        P.tt("dve", dst_r, xrb, pr, ALU.mult, [kr, "angc"], ["cp_r"])
        P.tt("dve", w4[:, :, 0:n, :], xib, pi, ALU.mult, [ki, "ang"], ["w4"])
        P.tt("dve", dst_r, dst_r, w4[:, :, 0:n, :], ALU.subtract, ["cp_r", "w4"], ["cp_r"])
        P.tt("dve", dst_i, xrb, pi, ALU.mult, [kr, "ang"], ["cp_i"])
        P.tt("dve", w4[:, :, 0:n, :], xib, pr, ALU.mult, [ki, "angc"], ["w4"])
        P.tt("dve", dst_i, dst_i, w4[:, :, 0:n, :], ALU.add, ["cp_i", "w4"], ["cp_i"])

    w4 = P.sb("w4", [128, 16, 9, 16], F32)
    WBr = P.sb("WBr", [128, 16, 9, 16], F32)
    WBi = P.sb("WBi", [128, 16, 9, 16], F32)
    Xb = [P.sb("Xb%d" % i, [128, 4, 2, 16], BF16) for i in range(2)]

    cprod(WBr[:, :, 0:8, :], WBi[:, :, 0:8, :], Bbr, Bbi, 41, 8, "Bbr", "Bbi")
    n_x = 0
    for ct in range(4):
        for r in range(8):
            for comp, WB in enumerate((WBr, WBi)):
                xb = Xb[n_x % 2]
                xk = ("Xb", n_x % 2)
                n_x += 1
                P.tt("dve", xb[:], WB[:, 4 * ct:4 * ct + 4, r, :].unsqueeze(2).broadcast_to([128, 4, 2, 16]),
                     emask[:].unsqueeze(1).unsqueeze(3).broadcast_to([128, 4, 2, 16]), ALU.mult,
                     ["cp_r", "cp_i", "emask"], [xk])
                tp, tk = next_tp("F" if n_x % 2 else "L")
                P.tr(tp[:, 0:128], xb[:].rearrange("p a b c -> p (a b c)"), identb[:], [xk, "identb"], tk)
                P.cp("act", Wst[:, ct, r, comp, :], tp[:, 0:128], tk, ["Wst"])
    cprod(WBr[:], WBi[:], Cqr, Cqi, 0, 9, "Cqr", "Cqi")
    for comp, (CA, sgn) in enumerate(((WBr, 1.0), (WBi, -1.0))):
        for e2 in range(2):
            P.ts("dve", Cmod[:, :, :, comp, e2 * 16:(e2 + 1) * 16], CA[:, :, 1:9, :], emask[:, e2:e2 + 1], sgn,
                 ALU.mult, ALU.mult, ["cp_r", "cp_i", "emask"], ["Cmod"])
    Bexp = P.sb("Bexp", [128, 16, 2, 128], BF16)
    CAexp = P.sb("CAexp", [128, 4, 2, 4, 128], BF16)
    P.op("pool", lambda e: e.memset(Bexp[:], 0.0), writes=["Bexp"])
    P.op("pool", lambda e: e.memset(CAexp[:], 0.0), writes=["CAexp"])
    for jj in range(4):
        for comp, Bb in enumerate((Bbr, Bbi)):
            for e2 in range(2):
                P.ts("dve", Bexp[:, jj::4, comp, jj * 32 + e2 * 16:jj * 32 + (e2 + 1) * 16], Bb[:, jj::4, :],
                     emask[:, e2:e2 + 1], None, ALU.mult, None, ["Bbr", "Bbi", "emask", "Bexp"], ["Bexp"])
    for ct in range(4):
        for h in range(2):
            for jj in range(4):
                for comp, (CA, sgn) in enumerate(((WBr, 1.0), (WBi, -1.0))):
                    for e2 in range(2):
                        P.ts("dve", CAexp[:, jj, comp, :, jj * 32 + e2 * 16:jj * 32 + (e2 + 1) * 16],
                             CA[:, 4 * ct + jj, 4 * h:4 * h + 4, :], emask[:, e2:e2 + 1], sgn, ALU.mult, ALU.mult,
                             ["cp_r", "cp_i", "emask", "CAexp"], ["CAexp"])
            bank, bk = next_gb("M")
            n = 0
            for jj in range(4):
                for comp in range(2):
                    P.mm(bank[:].rearrange("p (l c) -> p l c", c=128), Bexp[:, 4 * ct + jj, comp, :],
                         CAexp[:, jj, comp, :, :], n == 0, n == 7, ["Bexp", "CAexp"], bk)
                    n += 1
            P.cp("act", Kblk[:, ct, 4 * h:4 * h + 4, :], bank[:].rearrange("p (l c) -> p l c", c=128), bk, ["Kblk"])

    P.cp("dve", rhot[:], mag[:, :, 8:9].broadcast_to([128, 16, NCH]), ["mag"], ["rhot"])
    P.cp("dve", aLr_t[:], angc[:, :, NK - 1], ["angc"], ["aT"])
    P.cp("dve", aLi_t[:], ang[:, :, NK - 1], ["ang"], ["aT"])
    P.cp("dve", PcR_t[:], angc[:, :, 9:9 + NCH], ["angc"], ["PcR"])
    P.cp("dve", PcI_t[:], ang[:, :, 9:9 + NCH], ["ang"], ["PcI"])
    for c in range(NCH):
        P.cp("pool", PvR_t[:, :, c], angc[:, :, 9 + NCH - 1 - c], ["angc"], ["PvR"])
        P.cp("dve", PvI_t[:, :, c], ang[:, :, 9 + NCH - 1 - c], ["ang"], ["PvI"])
    P = new_prog("c_")
    tpb = [P.ps("tp%d" % i, [128, 1024], BF16) for i in range(2)]
    gb = [P.ps("gb%d" % i, [128, 512], F32) for i in range(6)]
    if WARM:
        P.pe_ghz = 2.0
        P.fill_dur = 80.0
        P.fill_gap = 300.0
        P.filler = lambda e: e.ldweights(identb[:])
    xt = [P.sb("xt%d" % i, [128, D], F32) for i in range(2)]
    xh16 = P.sb("xh16", [128, D], BF16)
    hh16 = P.sb("hh16", [128, D], BF16)
    r0s = [[P.sb("r0_%d" % i, [128, D], F32) for i in range(2)]] * 2
    lnsc = [dict(stats=P.sb("stats%d" % g, [128, 2, 6], F32), mv=P.sb("mv%d" % g, [128, 2], F32),
                 rstd=P.sb("rstd%d" % g, [128, 1], F32), nmr=P.sb("nmr%d" % g, [128, 1], F32)) for g in range(2)]
    x0T = P.sb("x0T", [128, KT, NST], BF16)
    qTs = [P.sb("qT%d" % p, [128, 4, NST], BF16) for p in range(2)]
    NW = 6
    kd = [P.sb("kd%d" % i, [128, NW * 128], BF16) for i in range(4)]
    vt = P.sb("vt", [128, NW, 2, 65], BF16)
    uPs = [P.sb("uP%d" % p, [128, 4, NCH, 2 * T - 1], BF16) for p in range(2)]
    catTs = [P.sb("catT", [128, KT, NST], BF16)] * 2
    cur = dict(par=0, w=[0, 0, 0], wn=0)

    def set_cur(par):
        cur.update(par=par, uP=uPs[par], qT=qTs[par], catT=catTs[par], r0=r0s[par])

    set_cur(0)
    S_r = P.sb("S_r", [128, 16, NCH], F32)
    S_i = P.sb("S_i", [128, 16, NCH], F32)
    M_r = P.sb("M_r", [128, 16, NCH], F32)
    M_i = P.sb("M_i", [128, 16, NCH], F32)
    G_r = P.sb("G_r", [128, 16, NCH], F32)
    G_i = P.sb("G_i", [128, 16, NCH], F32)
    H_r = P.sb("H_r", [128, 16, NCH], F32)
    H_i = P.sb("H_i", [128, 16, NCH], F32)
    red_r = P.sb("red_r", [128, 16], F32)
    red_i = P.sb("red_i", [128, 16], F32)
    Hprev = P.sb("Hprev", [128, 16, 2, NCH], BF16)
    Hc_r = P.sb("Hc_r", [128, 16], F32)
    Hc_i = P.sb("Hc_i", [128, 16], F32)
    Hs_r = P.sb("Hs_r", [128, 16, 4], F32)
    Hs_i = P.sb("Hs_i", [128, 16, 4], F32)
    Ho_r = P.sb("Ho_r", [128, 16, 4], F32)
    Ho_i = P.sb("Ho_i", [128, 16, 4], F32)
    inj_r = P.sb("inj_r", [128, 16, 4], F32)
    inj_i = P.sb("inj_i", [128, 16, 4], F32)
    it0 = P.sb("it0", [128, 16, 4], F32)
    it1 = P.sb("it1", [128, 16, 4], F32)
    y32 = P.sb("y32", [128, NST], F32)
    yg = P.sb("yg", [128, 4, NST], BF16)
    sg = P.sb("sg", [128, NST], BF16)
    PTa = [P.sb("PTa%d" % i, [128, 512], BF16) for i in range(2)]
    PTb = [P.sb("PTb%d" % i, [128, 512], BF16) for i in range(2)]
    den8 = P.sb("den8", [64, 8], F32)
    Atok = P.sb("Atok", [64, 8, 64], BF16)
    ckf = P.sb("ckf", [128, 128], F32)
    ckz = P.sb("ckz", [128, 4, 128], BF16)
    kc = [P.sb("kc%d" % i, [128, 4, 128], BF16) for i in range(4)]
    vc = P.sb("vc", [128, 4, 2, 65], BF16)
    kvo = P.sb("kvo", [128, 256], F32)
    z1s = [P.sb("z1_%d" % i, [128, D], F32) for i in range(2)]
    hTt = [P.sb("hTt%d" % i, [128, KT, 128], BF16) for i in range(2)]

    for kv in range(4):
        P.op("pool", lambda e, kv=kv: e.memset(kd[kv][:], 0.0), writes=[("kd", w) for w in range(NW)])
    P.op("pool", lambda e: e.memset(ckz[:], 0.0), writes=["ckz"])
    for p in range(2):
        P.op("pool", lambda e, p=p: e.memset(uPs[p][:], 0.0), writes=[("uT", p)])
    P.op("pool", lambda e: e.memset(vt[:], 1.0), writes=[("vt", w) for w in range(NW)])
    P.op("pool", lambda e: e.memset(vc[:], 1.0), writes=["vc"])
    P.op("pool", lambda e: e.memset(Hc_r[:], 0.0), writes=["Hin"])
    P.op("pool", lambda e: e.memset(Hc_i[:], 0.0), writes=["Hin"])
    cosM = cosMt[:]
    sinM = sinMt[:]

    ctr = dict(x=0, t=0)
    cast_pieces = []
    for r0_ in range(0, D, 32):
        cast_pieces.append((wgu_nat[r0_:r0_ + 32, :].rearrange("r (a c) -> r a c", c=1408),
                            w_gu_d[r0_:r0_ + 32, :].rearrange("r (a c) -> r a c", c=1408)))
    for r0_ in range(0, DFF, 128):
        cast_pieces.append((wdn_sc[r0_:r0_ + 128, :], w_dn_d[r0_:r0_ + 128, :]))
    NTILES = 2 * (NMAIN // NST) + 2 * (NTOK // NST)

    def release_casts(xkey):
        k = ctr.setdefault("cast", 0)
        t = ctr.setdefault("xtiles", 0)
        ctr["xtiles"] = t + 1
        left = max(1, NTILES - 4 - t)
        n = (len(cast_pieces) - k + left - 1) // left
        for kk in range(k, min(k + n, len(cast_pieces))):
            dst, src = cast_pieces[kk]
            P.dma("pool", "wcast%d" % (kk % 8), lambda e, dst=dst, src=src: e.dma_start(out=dst, in_=src),
                  reads=[xkey], writes=[("wcast", kk % 8)], detached=True, nbytes=720896)
        ctr["cast"] = min(k + n, len(cast_pieces))

    def ln_tile(src_ap, xbuf, xkey, resid, rkey, gi, bset=None):
        bset = gi if bset is None else bset
        stats, mv, rstd, nmr = (lnsc[bset][k] for k in ("stats", "mv", "rstd", "nmr"))
        kS, kM, kR, kN = (("ln", bset, k) for k in range(4))
        P.op("dve", lambda e: e.bn_stats(out=stats[:, 0, :], in_=src_ap[:, 0:512]), [xkey], [kS], 600.0)
        P.op("dve", lambda e: e.bn_stats(out=stats[:, 1, :], in_=src_ap[:, 512:1024]), [xkey], [kS], 600.0)
        P.op("dve", lambda e: e.bn_aggr(out=mv[:], in_=stats[:].rearrange("p a b -> p (a b)")), [kS], [kM])
        P.ts("dve", rstd[:], mv[:, 1:2], EPS, None, ALU.add, None, [kM], [kR])
        P.tt("pool", rstd[:], rstd[:], mhalf[:], ALU.pow, [kR, "mhalf"], [kR])
        P.stt(nmr[:], mv[:, 0:1], -1.0, rstd[:], ALU.mult, ALU.mult, [kM, kR], [kN])
        h16 = xh16 if bset == 0 else hh16
        P.act(h16[:], src_ap, AF.Identity, [xkey, kR, kN], [("h16", bset)], scale=rstd[:, 0:1], bias=nmr[:, 0:1])
        P.act(src_ap, src_ap, AF.Identity, [xkey, kR, kN], [xkey], scale=rstd[:, 0:1], bias=nmr[:, 0:1])
        if resid is not None:
            Gt, Bt = (G0a, B0a)
            P.tt("pool", resid[:], src_ap, Gt[:], ALU.mult, [xkey, ("tbl", 2 * gi)], [rkey])
            P.tt("pool", resid[:], resid[:], Bt[:], ALU.add, [rkey, ("tbl", 2 * gi + 1)], [rkey])

    def transpose_to(dstT, dkey, col0, gi, bset=None):
        bset = gi if bset is None else bset
        tp, tk = next_tp("F" if bset == 0 else "L")
        for kt in range(KT):
            h16 = xh16 if bset == 0 else hh16
            P.tr(tp[:, kt * 128:(kt + 1) * 128], h16[:, kt * 128:(kt + 1) * 128], identb[:], [("h16", bset), "identb"], tk)
        if gi == 1:
            P.cp("act", dstT[:].rearrange("p k t -> p (k t)"), tp[:, 0:KT * 128], tk, [dkey])
            return
        P.cp("act", dstT[:, :, col0:col0 + 128], tp[:, 0:KT * 128].rearrange("p (k t) -> p k t", t=128), tk, [dkey])
        return
        for kt in range(KT):
            eng = "act"
            if eng == "dve":
                P.ts("dve", dstT[:, kt, col0:col0 + 128], tp[:, kt * 128:(kt + 1) * 128], gcol[:, 2 * gi, kt:kt + 1],
                     gcol[:, 2 * gi + 1, kt:kt + 1], ALU.mult, ALU.add, tk + ["gcol"], [dkey])
            else:
                P.act(dstT[:, kt, col0:col0 + 128], tp[:, kt * 128:(kt + 1) * 128], AF.Identity, tk + ["gcol"], [dkey],
                      scale=gcol[:, 2 * gi, kt:kt + 1], bias=gcol[:, 2 * gi + 1, kt:kt + 1])

    def proj_fm(dst_ap, dkey, w_ap_fn, evac, chunked=False, bias=None):
        bank, bk = next_gb("F")
        for kt in range(KT):
            P.mm(bank[:, 0:NST], w_ap_fn(kt), x0T[:, kt, :], kt == 0, kt == KT - 1, ["x0T", "wres", "wres2"], bk)
        src = bank[:, 0:NST].rearrange("p (c r) -> p c r", r=T) if chunked else bank[:, 0:NST]
        if evac == "act":
            P.act(dst_ap, src, AF.Identity, bk + ["cfm"], [dkey], bias=bias)
        else:
            P.ts("dve", dst_ap, src, bias, None, ALU.add, None, bk + ["cfm"], [dkey])

    def front(xsrc, full, resid):
        if full:
            wa, wb = cur["wn"] % NW, (cur["wn"] + 1) % NW
            cur["w"] = [(cur["wn"] - 1) % NW, wa, wb]
            cur["wn"] += 2
        for m in range(2):
            i = ctr["x"] % 2
            ctr["x"] += 1
            P.dma("sp", "xt%d" % i, lambda e, i=i, m=m: e.dma_start(out=xt[i][:], in_=xsrc[m * 128:(m + 1) * 128, :]),
                  writes=[("xt", i)], nbytes=128 * D * 4)
            release_casts(("xt", i))
            bs = m if not resid else 0
            ln_tile(xt[i][:], xt[i], ("xt", i), cur["r0"][m] if resid else None, ("r0", m), 0, bset=bs)
            transpose_to(x0T, "x0T", m * 128, 0, bset=bs)
        ev = ["act", "dve"]
        n = 0
        for ct in range(4):
            proj_fm(cur["uP"][:, ct, :, T - 1:2 * T - 1], ("uT", cur["par"]),
                    lambda kt, ct=ct: w_inb[:, kt, 768 + ct * 128:768 + (ct + 1) * 128], ev[n % 2], chunked=True,
                    bias=cfm[:, 6 + ct:7 + ct])
            n += 1
        if full:
            for t4 in range(4):
                proj_fm(cur["qT"][:, t4, :], ("qT", cur["par"]), lambda kt, t4=t4: w_inb[:, kt, t4 * 128:(t4 + 1) * 128], ev[n % 2],
                        bias=cfm[:, t4:t4 + 1])
                n += 1
            for tile_b in range(2):
                bank, bk = next_gb("F")
                for kt in range(KT):
                    lw = w_kd[:, kt, :] if tile_b else w_inb[:, kt, 512:640]
                    P.mm(bank[:, 0:NST], lw, x0T[:, kt, :], kt == 0, kt == KT - 1, ["x0T", "wres", "wres2"], bk)
                bias = ckd[:, 0:1] if tile_b else cfm[:, 4:5]
                variants = ((2, 0, 64), (1, 64, 128)) if tile_b else ((0, 0, 64), (3, 64, 128))
                for (v, lo, hi) in variants:
                    for m in range(2):
                        wi = cur["w"][1 + m]
                        P.ts("dve", kd[v][lo:hi, wi * 128:(wi + 1) * 128], bank[lo:hi, m * 128:(m + 1) * 128],
                             bias[lo:hi, :], None, ALU.add, None, bk + ["ckd", "cfm"], [("kd", wi)])
                n += 1
            for m in range(2):
                bank, bk = next_gb("F")
                for kt in range(KT):
                    P.mm(bank[:, 0:128], x0T[:, kt, m * 128:(m + 1) * 128], w_inb[:, kt, 640:768], kt == 0, kt == KT - 1,
                         ["x0T", "wres"], bk)
                P.tt("dve", vt[:, cur["w"][1 + m], :, 0:64], bank[:, 0:128].rearrange("p (a b) -> p a b", b=64),
                     ckvb[:, 128:256].rearrange("p (a b) -> p a b", b=64), ALU.add, bk + ["ckvb"],
                     [("vt", cur["w"][1 + m])])
                n += 1

    def s_compute():
        banks = [next_gb("S"), next_gb("S"), next_gb("M"), next_gb("M")]
        for ct in range(4):
            for comp in range(2):
                col = (ct * 2 + comp) * NCH
                for r in range(T):
                    for jj in range(4):
                        bank, bk = banks[jj]
                        rhs = cur["uP"][jj * 32:(jj + 1) * 32, ct, :, T - 1 + r]
                        P.mm(bank[:, col:col + NCH], Wst[jj * 32:(jj + 1) * 32, ct, r, comp, :], rhs, r == 0, r == T - 1,
                             [("uT", cur["par"]), "Wst"], bk, tp=(jj * 32, 0))
        for jj in range(4):
            bank, bk = banks[jj]
            bv = bank[:, 0:256].rearrange("p (a b c) -> p a b c", a=4, b=2)
            P.cp("act", S_r[:, jj::4, :], bv[:, :, 0, :], bk, ["S_r"])
            P.cp("act", S_i[:, jj::4, :], bv[:, :, 1, :], bk, ["S_i"])

    def ssm_states(seq_starts, inj_fn):
        s_compute()
        ns = len(seq_starts)
        step = NCH // ns
        hr, hi = inj_fn()
        if ns > 1:
            aTr = PcR_t[:, :, 1:2].broadcast_to([128, 16, ns])
            aTi = PcI_t[:, :, 1:2].broadcast_to([128, 16, ns])
            cmul_tab(inj_r[:], inj_i[:], aTr, aTi, hr, hi, it0[:], it1[:], "ij")
            ir, ii = inj_r[:], inj_i[:]
        else:
            cmul_tab(inj_r[:, :, 0], inj_i[:, :, 0], PcR_t[:, :, 1], PcI_t[:, :, 1], hr, hi, it0[:, :, 0], it1[:, :, 0], "ij")
            ir, ii = inj_r[:, :, 0:1], inj_i[:, :, 0:1]
        P.tt("dve", S_r[:, :, ::step], S_r[:, :, ::step], ir, ALU.add, ["S_r", "ijr"], ["S_r"])
        P.tt("dve", S_i[:, :, ::step], S_i[:, :, ::step], ii, ALU.add, ["S_i", "iji"], ["S_i"])
        P.tt("dve", M_r[:], S_r[:], cosM, ALU.mult, ["S_r", "cosT"], ["M_r"])
        P.tt("pool", G_r[:], S_i[:], sinM, ALU.mult, ["S_i", "sinT"], ["G_r"])
        P.tt("dve", M_r[:], M_r[:], G_r[:], ALU.add, ["M_r", "G_r"], ["M_r"])
        P.tt("dve", M_i[:], S_i[:], cosM, ALU.mult, ["S_i", "cosT"], ["M_i"])
        P.tt("pool", G_i[:], S_r[:], sinM, ALU.mult, ["S_r", "sinT"], ["G_i"])
        P.tt("dve", M_i[:], M_i[:], G_i[:], ALU.subtract, ["M_i", "G_i"], ["M_i"])
        bounds = list(seq_starts) + [NCH]
        for j in range(16):
            for (Mx, Gx, key, gkey) in ((M_r, G_r, "M_r", "G_r"), (M_i, G_i, "M_i", "G_i")):
                for s in range(ns):
                    a, b = bounds[s], bounds[s + 1]
                    P.op("dve", lambda e, Mx=Mx, Gx=Gx, j=j, a=a, b=b: e.tensor_tensor_scan(
                        out=Gx[:, j, a:b], data0=rhot[:, j, a:b], data1=Mx[:, j, a:b], initial=0.0,
                        op0=ALU.mult, op1=ALU.add), [key, "rhot"], [gkey], 230.0)
        P.tt("dve", H_r[:], G_r[:], cosM, ALU.mult, ["G_r", "cosT"], ["H_r"])
        P.tt("pool", M_r[:], G_i[:], sinM, ALU.mult, ["G_i", "sinT"], ["M_r"])
        P.tt("dve", H_r[:], H_r[:], M_r[:], ALU.subtract, ["H_r", "M_r"], ["H_r"])
        P.tt("dve", H_i[:], G_i[:], cosM, ALU.mult, ["G_i", "cosT"], ["H_i"])
        P.tt("pool", M_i[:], G_r[:], sinM, ALU.mult, ["G_r", "sinT"], ["M_i"])
        P.tt("dve", H_i[:], H_i[:], M_i[:], ALU.add, ["H_i", "M_i"], ["H_i"])

    def ssm_end_state_only():
        s_compute()
        P.tt("dve", M_r[:], S_r[:], PvR_t[:], ALU.mult, ["S_r", "PvR"], ["M_r"])
        P.tt("pool", G_r[:], S_i[:], PvI_t[:], ALU.mult, ["S_i", "PvI"], ["G_r"])
        P.tt("dve", M_r[:], M_r[:], G_r[:], ALU.subtract, ["M_r", "G_r"], ["M_r"])
        P.tt("dve", M_i[:], S_i[:], PvR_t[:], ALU.mult, ["S_i", "PvR"], ["M_i"])
        P.tt("pool", G_i[:], S_r[:], PvI_t[:], ALU.mult, ["S_r", "PvI"], ["G_i"])
        P.tt("dve", M_i[:], M_i[:], G_i[:], ALU.add, ["M_i", "G_i"], ["M_i"])
        P.op("dve", lambda e: e.tensor_reduce(out=red_r[:], in_=M_r[:], op=ALU.add, axis=mybir.AxisListType.X),
             ["M_r"], ["red_r"], 700.0)
        P.op("dve", lambda e: e.tensor_reduce(out=red_i[:], in_=M_i[:], op=ALU.add, axis=mybir.AxisListType.X),
             ["M_i"], ["red_i"], 700.0)
        cmul_tab(inj_r[:, :, 0], inj_i[:, :, 0], aLr_t[:], aLi_t[:], Hc_r[:], Hc_i[:], it0[:, :, 0], it0[:, :, 1], "cy")
        P.tt("dve", Hc_r[:], inj_r[:, :, 0], red_r[:], ALU.add, ["cyr", "red_r"], ["Hin"])
        P.tt("dve", Hc_i[:], inj_i[:, :, 0], red_i[:], ALU.add, ["cyi", "red_i"], ["Hin"])

    def cmul_tab(dst_r, dst_i, tr_, ti_, hr, hi, tmp_r, tmp_i, kd_):
        P.tt("dve", dst_r, tr_, hr, ALU.mult, ["PcR", "aT", "Hin"], [kd_ + "r"])
        P.tt("pool", tmp_r, ti_, hi, ALU.mult, ["PcI", "aT", "Hin"], [kd_ + "tr"])
        P.tt("dve", dst_r, dst_r, tmp_r, ALU.subtract, [kd_ + "r", kd_ + "tr"], [kd_ + "r"])
        P.tt("dve", dst_i, tr_, hi, ALU.mult, ["PcR", "aT", "Hin"], [kd_ + "i"])
        P.tt("pool", tmp_i, ti_, hr, ALU.mult, ["PcI", "aT", "Hin"], [kd_ + "ti"])
        P.tt("dve", dst_i, dst_i, tmp_i, ALU.add, [kd_ + "i", kd_ + "ti"], [kd_ + "i"])

    def ssm_carry_prompt():
        P.cp("dve", Hc_r[:], H_r[:, :, NCH - 1], ["H_r", "Hprev", "ijr", "iji"], ["Hin"])
        P.cp("dve", Hc_i[:], H_i[:, :, NCH - 1], ["H_i", "Hprev", "ijr", "iji"], ["Hin"])

    def ssm_out(seq_starts, inj_fn):
        hr, hi = inj_fn()
        ns = len(seq_starts)
        step = NCH // ns
        P.cp("act", Hprev[:, :, 0, 1:NCH], H_r[:, :, 0:NCH - 1], ["H_r"], ["Hprev"])
        P.cp("pool", Hprev[:, :, 1, 1:NCH], H_i[:, :, 0:NCH - 1], ["H_i"], ["Hprev"])
        hr3 = hr if ns > 1 else hr.unsqueeze(2)
        hi3 = hi if ns > 1 else hi.unsqueeze(2)
        P.cp("dve", Hprev[:, :, 0, ::step], hr3, ["Hin", "Hprev"], ["Hprev"])
        P.cp("dve", Hprev[:, :, 1, ::step], hi3, ["Hin", "Hprev"], ["Hprev"])
        for ct in range(4):
            bank, bk = next_gb("Sh")
            yv = bank[:, 0:NST].rearrange("p (c r) -> p c r", r=T)
            P.mm(bank[:, 0:NST], Kblk[:, ct, 0, :], cur["uP"][:, ct, :, T - 1:2 * T - 1], True, False,
                 [("uT", cur["par"]), "Kblk"], bk)
            for r in range(T):
                for comp in range(2):
                    for jj in range(4):
                        j = 4 * ct + jj
                        P.mm(yv[jj * 32:(jj + 1) * 32, :, r], Cmod[:, j, r, comp, :], Hprev[:, j, comp, :],
                             False, False, ["Cmod", "Hprev"], bk, tp=(0, jj * 32))
            for l in range(1, T):
                P.mm(bank[:, 0:NST], Kblk[:, ct, l, :], cur["uP"][:, ct, :, T - 1 - l:2 * T - 1 - l], False, l == T - 1,
                     [("uT", cur["par"]), "Kblk"], bk)
            P.stt(y32[:].rearrange("p (c r) -> p c r", r=T), cur["uP"][:, ct, :, T - 1:2 * T - 1], dcol[:, ct:ct + 1], yv,
                  ALU.mult, ALU.add, [("uT", cur["par"]), "dcol"] + bk, ["y32"])
            P.act(yg[:, ct, :], y32[:], AF.Gelu, ["y32"], ["yg"])
        for ot in range(4):
            bank, bk = next_gb("Sh")
            for ct in range(4):
                P.mm(bank[:, 0:NST], w_glub[:, ct, ot * 128:(ot + 1) * 128], yg[:, ct, :], ct == 0, ct == 3,
                     ["yg", "wres"], bk)
            P.act(sg[:], bank[:, 0:NST], AF.Tanh, bk + ["bglu"], ["sg"], scale=0.5, bias=bglu[:, ot:ot + 1])
            P.stt(cur["catT"][:, 4 + ot, :], sg[:], 1.0, yg[:, ot, :], ALU.add, ALU.mult, ["yg", "sg"], ["catT_s"])

    def attn_chunk(cl, blocks):
        i = ctr["t"] % 2
        ctr["t"] += 1
        P.prio_bias = -1500
        pts = []
        for bi, (k_fn, v_ap, bias_ap, kkeys) in enumerate(blocks):
            bank, bk = next_gb("Q")
            for kv in range(2):
                for e2 in range(2):
                    v = 2 * kv + e2
                    P.mm(bank[:, v * 128:(v + 1) * 128], k_fn(v), cur["qT"][:, 2 * kv:2 * kv + 2, cl * 64:(cl + 1) * 64],
                         True, True, [("qT", cur["par"])] + kkeys, bk)
            pt = (PTa if bi == 0 else PTb)[i]
            pk = ("PT", bi, i)
            P.act(pt[:], bank[:], AF.Exp, bk + ["hbias", "zbias"], [pk], scale=0.125, bias=bias_ap)
            pts.append((pt, pk))
        stop_here("at1")
        obanks = [next_gb("M"), next_gb("M")]
        for hq in range(8):
            kv, g = hq // 4, hq % 4
            i2, e2 = g // 2, g % 2
            col = ((kv * 2 + e2) * 2 + i2) * 64
            bank, bk = obanks[hq // 4]
            o = bank[0:64, 0:260].rearrange("p (h c) -> p h c", c=65)[:, hq % 4, :]
            for bi, (k_fn, v_ap, bias_ap, kkeys) in enumerate(blocks):
                pt, pk = pts[bi]
                P.mm(o, pt[:, col:col + 64], v_ap[:, kv, :], bi == 0, bi == len(blocks) - 1,
                     [pk, "vc"] + kkeys, bk)
        stop_here("at2")
        for hb in range(2):
            bank, bk = obanks[hb]
            ov = bank[0:64, 0:260].rearrange("p (h c) -> p h c", c=65)
            P.tt("dve", den8[:, hb * 4:(hb + 1) * 4], ov[:, :, 64], sinkexp[0:64, hb * 4:(hb + 1) * 4], ALU.add,
                 bk + ["sinkexp"], ["den8"])
        P.op("dve", lambda e: e.reciprocal(out=den8[:], in_=den8[:]), ["den8"], ["den8"])
        for hb in range(2):
            bank, bk = obanks[hb]
            ov = bank[0:64, 0:260].rearrange("p (h c) -> p h c", c=65)
            P.tt("dve", Atok[:, hb * 4:(hb + 1) * 4, :], ov[:, :, 0:64],
                 den8[:, hb * 4:(hb + 1) * 4].unsqueeze(2).broadcast_to([64, 4, 64]), ALU.mult, bk + ["den8"], ["Atok"])
        stop_here("at3")
        tp, tk = next_tp("A")
        for t4 in range(4):
            P.tr(tp[:, t4 * 64:(t4 + 1) * 64], Atok[:, 2 * t4:2 * t4 + 2, :].rearrange("p a b -> p (a b)"),
                 identb[0:64, 0:64], ["Atok", "identb"], tk)
        P.cp("act", cur["catT"][:, 0:4, cl * 64:(cl + 1) * 64], tp[:, 0:256].rearrange("p (a b) -> p a b", b=64),
             tk, [("catT_a", cl)])
        P.prio_bias = 0

    def mix_ln1(tok0, m):
        banks = [next_gb("M"), next_gb("M")]
        for h in range(2):
            bank, bk = banks[h]
            for kt in range(KT):
                P.mm(bank[:], cur["catT"][:, kt, m * 128:(m + 1) * 128], w_outb[:, kt, h * 512:(h + 1) * 512], kt == 0,
                     kt == KT - 1, [("catT_a", 2 * m), ("catT_a", 2 * m + 1),
                                    "catT_s", "wres"], bk)
        i = ctr["x"] % 2
        ctr["x"] += 1
        z1 = z1s[i]
        zk = ("z1", i)
        for h in range(2):
            bank, bk = banks[h]
            P.tt("dve", z1[:, h * 512:(h + 1) * 512], bank[:], cur["r0"][m][:, h * 512:(h + 1) * 512], ALU.add,
                 bk + [("r0", m)], [zk])
        ln_tile(z1[:], z1, zk, None, None, 1)
        transpose_to(hTt[i], ("hTt", i), 0, 1)
        P.dma("act", "r1o%d" % i, lambda e, i=i, z1=z1: e.dma_start(out=r1_sc[tok0:tok0 + 128, :], in_=z1[:]),
              reads=[zk], final=True, nbytes=128 * D * 4)
        P.dma("act", "hTo%d" % i, lambda e, i=i: e.dma_start(out=hT_sc[:, :, tok0:tok0 + 128], in_=hTt[i][:]),
              reads=[("hTt", i)], final=True)

    for s in range(NMAIN // NST):
        last = (s == NMAIN // NST - 1)
        set_cur(s % 2)
        front(xprev[s * NST:(s + 1) * NST, :], last, False)
        ssm_end_state_only()
    stop_here("c1")
    P.ts("dve", Hc_r[:], Hc_r[:], flag[:, 0:1], None, ALU.mult, None, ["Hin", "flag"], ["Hin"])
    P.ts("dve", Hc_i[:], Hc_i[:], flag[:, 0:1], None, ALU.mult, None, ["Hin", "flag"], ["Hin"])

    def kwin(w):
        return lambda v: kd[v][:, w * 128:(w + 1) * 128]

    def wkeys(w):
        return [("kd", w), ("vt", w)]

    for s in range(NMAIN // NST):
        set_cur(s % 2)
        P.prio_bias = -700
        front(xmain[s * NST:(s + 1) * NST, :], True, True)
        P.prio_bias = 0
        ssm_states([0], lambda: (Hc_r[:], Hc_i[:]))
        ssm_out([0], lambda: (Hc_r[:], Hc_i[:]))
        ssm_carry_prompt()
        if s == 0:
            stop_here("c2a")
        W = list(cur["w"])
        for cl in range(4):
            m = cl // 2
            halo = (s == 0 and m == 0)
            w0, w1 = W[m], W[m + 1]
            if cl % 2 == 0:
                blocks = [(kwin(w0), vt[:, w0, :, :], hbias[:, 0:1] if halo else zbias[:, 0:1], wkeys(w0)),
                          (kwin(w1), vt[:, w1, :, :], hbias[:, 3:4], wkeys(w1))]
            else:
                blocks = [(kwin(w0), vt[:, w0, :, :], hbias[:, 1:2] if halo else hbias[:, 2:3], wkeys(w0)),
                          (kwin(w1), vt[:, w1, :, :], zbias[:, 0:1], wkeys(w1))]
            attn_chunk(cl, blocks)
        if s == 0:
            stop_here("c2b")
        for m in range(2):
            mix_ln1(s * NST + m * 128, m)
        if s == 0:
            stop_here("c2")
        if s == NMAIN // NST - 1:
            bank, bk = next_gb("F")
            for kt in range(KT):
                P.mm(bank[:, 0:256], x0T[:, kt, 128:256], w_inb[:, kt, 512:768], kt == 0, kt == KT - 1,
                     ["x0T", "wres"], bk)
            P.tt("dve", kvo[:], bank[:, 0:256], ckvb[:], ALU.add, bk + ["ckvb"], ["kvo"])
            P.dma("sp", "kvo", lambda e: e.dma_start(out=kwin_p, in_=kvo[:, 0:128]), reads=["kvo"], final=True)
            P.dma("sp", "kvo", lambda e: e.dma_start(out=vwin_p, in_=kvo[:, 128:256]), reads=["kvo"], final=True)
            P.dma("sp", "sso", lambda e: e.dma_start(out=sre_p, in_=Hc_r[:]), reads=["Hin"], final=True)
            P.dma("sp", "sso", lambda e: e.dma_start(out=sim_p, in_=Hc_i[:]), reads=["Hin"], final=True)

    stop_here("c3")
    ld("c3", Hs_r[:], str_d, "Hin")
    ld("c3", Hs_i[:], sti_d, "Hin")
    for sq in range(4):
        P.dma("sp", "ck", lambda e, sq=sq: e.dma_start(out=ckf[:], in_=ck_d[sq]), writes=["ckf"])
        for kv in range(2):
            for e2 in range(2):
                v = 2 * kv + e2
                P.cp("dve", ckz[:, v, e2 * 64:(e2 + 1) * 64], ckf[:, kv * 64:(kv + 1) * 64], ["ckf", "ckz"], ["ckz"])
        tp, tk = next_tp("L")
        for v in range(4):
            P.tr(tp[:, v * 128:(v + 1) * 128], ckz[:, v, :], identb[:], ["ckz", "identb"], tk)
        for v in range(4):
            P.cp("act", kc[v][:, sq, :], tp[:, v * 128:(v + 1) * 128], tk, ["kc"])
        P.dma("sp", "wino", lambda e, sq=sq: e.dma_start(out=kwin_s[sq, 0:64, :], in_=ck_d[sq, 64:128, :]), final=True)
        P.dma("sp", "wino", lambda e, sq=sq: e.dma_start(out=vwin_s[sq, 0:64, :], in_=cv_d[sq, 64:128, :]), final=True)
        P.dma("sp", "ck", lambda e, sq=sq: e.dma_start(out=ckf[:], in_=cv_d[sq]), writes=["ckf"])
        P.cp("dve", vc[:, sq, :, 0:64], ckf[:].rearrange("p (a b) -> p a b", b=64), ["ckf"], ["vc"])
    set_cur(0)
    front(xsamp, True, True)
    W = list(cur["w"])
    starts = [0, 8, 16, 24]
    ssm_states(starts, lambda: (Hs_r[:], Hs_i[:]))
    ssm_out(starts, lambda: (Hs_r[:], Hs_i[:]))
    P.cp("dve", Ho_r[:], H_r[:, :, 7::8], ["H_r"], ["Ho"])
    P.cp("dve", Ho_i[:], H_i[:, :, 7::8], ["H_i"], ["Ho"])
    P.dma("sp", "sso", lambda e: e.dma_start(out=sre_s, in_=Ho_r[:]), reads=["Ho"], final=True)
    P.dma("sp", "sso", lambda e: e.dma_start(out=sim_s, in_=Ho_i[:]), reads=["Ho"], final=True)
    for cl in range(4):
        m = cl // 2
        w1 = W[m + 1]
        cache_blk = (lambda v, cl=cl: kc[v][:, cl, :], vc[:, cl, :, :], zbias[:, 0:1], ["kc"])
        own = (kwin(w1), vt[:, w1, :, :], hbias[:, 3:4] if cl % 2 == 0 else hbias[:, 2:3], wkeys(w1))
        attn_chunk(cl, [cache_blk, own])
    for m in range(2):
        mix_ln1(NMAIN + m * 128, m)
        bank, bk = next_gb("F")
        for kt in range(KT):
            P.mm(bank[:, 0:256], x0T[:, kt, m * 128:(m + 1) * 128], w_inb[:, kt, 512:768], kt == 0, kt == KT - 1,
                 ["x0T", "wres"], bk)
        P.tt("dve", kvo[:], bank[:, 0:256], ckvb[:], ALU.add, bk + ["ckvb"], ["kvo"])
        for q2 in range(2):
            sq = 2 * m + q2
            P.dma("sp", "kvo", lambda e, sq=sq, q2=q2: e.dma_start(out=kwin_s[sq, 64:128, :],
                                                                 in_=kvo[q2 * 64:(q2 + 1) * 64, 0:128]),
                  reads=["kvo"], final=True)
            P.dma("sp", "kvo", lambda e, sq=sq, q2=q2: e.dma_start(out=vwin_s[sq, 64:128, :],
                                                                 in_=kvo[q2 * 64:(q2 + 1) * 64, 128:256]),
                  reads=["kvo"], final=True)
    stop_here("c4")
    P.emit()
    nc.all_engine_barrier()
    ES.close()
    ES = ExitStack()

    Q = Prog(nc, "d_", ES)
    Q.pre_waits = list(DETACHED.values())
    gq = [Q.ps("gb%d" % i, [128, 512], F32) for i in range(8)]
    stq = dict(gb=0, w=0, d=0, t=0)

    def qnext():
        i = stq["gb"] % 8
        stq["gb"] += 1
        return gq[i], [("gb", i)]

    NH = NTOK // 2
    G2 = Q.sb("G2", [128, D], F32)
    B2 = Q.sb("B2", [128, D], F32)
    Q.dma("sp", "cg2", lambda e: e.dma_start(out=G2[:], in_=gbc_d[4]), writes=["G2"])
    Q.dma("sp", "cb2", lambda e: e.dma_start(out=B2[:], in_=gbc_d[5]), writes=["B2"])
    G1q = Q.sb("G1q", [128, D], F32)
    B1q = Q.sb("B1q", [128, D], F32)
    Q.dma("sp", "cg1", lambda e: e.dma_start(out=G1q[:], in_=gbc_d[2]), writes=["G1q"])
    Q.dma("sp", "cb1", lambda e: e.dma_start(out=B1q[:], in_=gbc_d[3]), writes=["B1q"])
    Q.op("act", lambda e: e.mul(out=G1q[:], in_=G1q[:], mul=ALPHA), ["G1q"], ["G1q"], 1100.0)
    Q.op("act", lambda e: e.mul(out=B1q[:], in_=B1q[:], mul=ALPHA), ["B1q"], ["B1q"], 1100.0)
    gcol2 = Q.sb("gcol2", [128, 6, KT], F32)
    Q.dma("sp", "cgc", lambda e: e.dma_start(out=gcol2[:], in_=gcol_d), writes=["gcol2"])
    hT = Q.sb("hT", [128, KT, NH], BF16)
    actT = Q.sb("actT", [128, NJ, NH], BF16)
    wdn = Q.sb("wdn", [128, NJ, D], BF16)
    wgu = [Q.sb("wgu%d" % i, [128, KT, 256], BF16) for i in range(3)]
    sgq = [Q.sb("sgq%d" % i, [128, 512], BF16) for i in range(2)]
    r1q = [Q.sb("r1q%d" % i, [128, D], F32) for i in range(2)]
    z2s = [Q.sb("z2_%d" % i, [128, D], F32) for i in range(2)]
    yo = [Q.sb("yo%d" % i, [128, D], F32) for i in range(2)]
    stats2 = Q.sb("stats2", [128, 2, 6], F32)
    mv2 = Q.sb("mv2", [128, 2], F32)
    rstd2 = Q.sb("rstd2", [128, 1], F32)
    nmr2 = Q.sb("nmr2", [128, 1], F32)
    mhalf2 = Q.sb("mhalf2", [128, 1], F32)
    Q.op("pool", lambda e: e.memset(mhalf2[:], -0.5), writes=["mhalf2"])
    for half in range(2):
        t0g = half * NH
        for kt in range(KT):
            Q.dma("sp", "hT%d" % kt, lambda e, kt=kt, t0g=t0g: e.dma_start(out=hT[:, kt, :], in_=hT_sc[:, kt, t0g:t0g + NH]),
                  writes=[("hT", kt)], nbytes=128 * NH * 2)
            Q.act(hT[:, kt, :], hT[:, kt, :], AF.Identity, [("hT", kt), "gcol2"], [("hT", kt)],
                  scale=gcol2[:, 2, kt:kt + 1], bias=gcol2[:, 3, kt:kt + 1])
        for j in range(NJ):
            wi = stq["w"] % 3
            stq["w"] += 1
            wnat = wgu_nat.rearrange("(k p) c -> p k c", p=128)
            for g in range(2):
                Q.dma("sp", "wgu%d_%d" % (wi, g), lambda e, wi=wi, j=j, g=g: e.dma_start(
                    out=wgu[wi][:, :, g * 128:(g + 1) * 128],
                    in_=wnat[:, :, g * DFF + j * 128:g * DFF + (j + 1) * 128]),
                    writes=[("wgu", wi, g)], nbytes=128 * KT * 128 * 2)
            if half == 0 and j == 2:
                for jq in range(0, NJ, 2):
                    Q.dma("sp", "wdn%d" % jq, lambda e, jq=jq: e.dma_start(
                        out=wdn[:, jq:jq + 2, :], in_=wdn_sc[jq * 128:(jq + 2) * 128, :].rearrange("(j p) c -> p j c", p=128)),
                        writes=[("wdn", jq), ("wdn", jq + 1)], nbytes=2 * 128 * D * 2)
            for (c0, cn) in ((0, 512), (512, 512), (1024, 128)):
                bg, bgk = qnext()
                bu, buk = qnext()
                for kt in range(KT):
                    Q.mm(bg[:, 0:cn], wgu[wi][:, kt, 0:128], hT[:, kt, c0:c0 + cn], kt == 0, kt == KT - 1,
                         [("wgu", wi, 0), ("hT", kt)], bgk)
                for kt in range(KT):
                    Q.mm(bu[:, 0:cn], wgu[wi][:, kt, 128:256], hT[:, kt, c0:c0 + cn], kt == 0, kt == KT - 1,
                         [("wgu", wi, 1), ("hT", kt)], buk)
                si = stq["d"] % 2
                stq["d"] += 1
                Q.act(sgq[si][:, 0:cn], bg[:, 0:cn], AF.Silu, bgk, [("sgq", si)])
                Q.tt("dve", actT[:, j, c0:c0 + cn], bu[:, 0:cn], sgq[si][:, 0:cn], ALU.mult, buk + [("sgq", si)],
                     [("actT", j, c0)])
        for tt_ in range(NH // 128):
            tok0 = t0g + tt_ * 128
            ri = stq["t"] % 2
            stq["t"] += 1
            Q.dma("sp", "r1q%d" % ri, lambda e, ri=ri, tok0=tok0: e.dma_start(out=r1q[ri][:], in_=r1_sc[tok0:tok0 + 128, :]),
                  writes=[("r1q", ri)])
            Q.tt("dve", r1q[ri][:], r1q[ri][:], G1q[:], ALU.mult, [("r1q", ri), "G1q"], [("r1q", ri)])
            Q.tt("pool", r1q[ri][:], r1q[ri][:], B1q[:], ALU.add, [("r1q", ri), "B1q"], [("r1q", ri)])
            banks = [qnext(), qnext()]
            for h in range(2):
                bank, bk = banks[h]
                for j in range(NJ):
                    Q.mm(bank[:], actT[:, j, tt_ * 128:(tt_ + 1) * 128], wdn[:, j, h * 512:(h + 1) * 512], j == 0,
                         j == NJ - 1, [("actT", j, (tt_ // 4) * 512), ("wdn", j)], bk)
            z2 = z2s[ri]
            zk = ("z2", ri)
            for h in range(2):
                bank, bk = banks[h]
                Q.tt("dve", z2[:, h * 512:(h + 1) * 512], bank[:], r1q[ri][:, h * 512:(h + 1) * 512], ALU.add,
                     bk + [("r1q", ri)], [zk])
            Q.op("dve", lambda e, z2=z2: e.bn_stats(out=stats2[:, 0, :], in_=z2[:, 0:512]), [zk], ["stats2"], 600.0)
            Q.op("dve", lambda e, z2=z2: e.bn_stats(out=stats2[:, 1, :], in_=z2[:, 512:1024]), [zk], ["stats2"], 600.0)
            Q.op("dve", lambda e: e.bn_aggr(out=mv2[:], in_=stats2[:].rearrange("p a b -> p (a b)")), ["stats2"], ["mv2"])
            Q.ts("dve", rstd2[:], mv2[:, 1:2], EPS, None, ALU.add, None, ["mv2"], ["rstd2"])
            Q.tt("pool", rstd2[:], rstd2[:], mhalf2[:], ALU.pow, ["rstd2", "mhalf2"], ["rstd2"])
            Q.stt(nmr2[:], mv2[:, 0:1], -1.0, rstd2[:], ALU.mult, ALU.mult, ["mv2", "rstd2"], ["nmr2"])
            Q.act(z2[:], z2[:], AF.Identity, [zk, "rstd2", "nmr2"], [zk], scale=rstd2[:, 0:1], bias=nmr2[:, 0:1])
            Q.tt("pool", yo[ri][:], z2[:], G2[:], ALU.mult, [zk, "G2"], [("yo", ri)])
            Q.tt("dve", yo[ri][:], yo[ri][:], B2[:], ALU.add, [("yo", ri), "B2"], [("yo", ri)])
            if tok0 < NMAIN:
                dst = y_main[tok0:tok0 + 128, :]
            else:
                dst = y_samp[tok0 - NMAIN:tok0 - NMAIN + 128, :]
            Q.dma("act", "yo%d" % ri, lambda e, ri=ri, dst=dst: e.dma_start(out=dst, in_=yo[ri][:]),
                  reads=[("yo", ri)], final=True)
    Q.emit()
    ES.close()


def make_in_maps(x_prompt, x_sample, cache_win_k, cache_win_v, state_ssm_re, state_ssm_im,
           ln_in_g, ln_in_b, w_in, attn_sinks, ssm_lambda_re, ssm_lambda_im, ssm_log_step,
           ssm_b_re, ssm_b_im, ssm_c_re, ssm_c_im, ssm_d, w_glu, b_glu, w_out,
           ln1_g, ln1_b, w_gate_up, w_down, ln2_g, ln2_b):
    f = lambda a: np.ascontiguousarray(np.asarray(a, dtype=np.float32))
    x_prompt, x_sample = f(x_prompt), f(x_sample)

    def st_layout(a):
        a = f(a)
        a = a.reshape((16, 2, 64) + a.shape[2:])
        a = np.moveaxis(a, 0, 2)
        return np.ascontiguousarray(a.reshape((128, 16) + a.shape[3:]))

    gvecs = [f(ln_in_g), f(ln_in_b), f(ln1_g)[0], f(ln1_b)[0], f(ln2_g)[0], f(ln2_b)[0]]
    gbc = np.ascontiguousarray(np.stack([np.broadcast_to(v[None, :], (128, D)) for v in gvecs]))
    gcol = np.ascontiguousarray(np.stack([v.reshape(KT, 128).T for v in gvecs], axis=1))
    shared = {
        "ident": np.eye(128, dtype=np.float32),
        "emask": np.ascontiguousarray((np.arange(128)[:, None] // 64 == np.arange(2)[None, :]).astype(np.float32)),
        "gbc": gbc, "gcol": gcol,
        "kvec": np.ascontiguousarray(np.broadcast_to(np.array(KLIST, np.float32)[None, :], (128, NK))),
        "lamr": st_layout(ssm_lambda_re[0]), "lami": st_layout(ssm_lambda_im[0]),
        "lstep": st_layout(np.broadcast_to(f(ssm_log_step)[0][:, None], (32, 64))),
        "bqr": st_layout(ssm_b_re[0]), "bqi": st_layout(ssm_b_im[0]),
        "cqr": st_layout(np.swapaxes(f(ssm_c_re)[0], 1, 2)), "cqi": st_layout(np.swapaxes(f(ssm_c_im)[0], 1, 2)),
        "dcol": np.ascontiguousarray(f(ssm_d)[0].reshape(4, 128).T),
        "bglucol": np.ascontiguousarray(f(b_glu)[0].reshape(4, 128).T),
        "sinkbc": np.ascontiguousarray(np.broadcast_to(f(attn_sinks)[0][None, :], (128, 8))),
        "w_in": f(w_in)[0], "w_glu": f(w_glu)[0], "w_out": f(w_out)[0],
        "w_gate_up": f(w_gate_up)[0], "w_down": f(w_down)[0],
    }
    in_maps = []
    for c in range(8):
        b, half = c // 2, c % 2
        m = dict(shared)
        m["xprev"] = np.ascontiguousarray(x_prompt[b, 0:NMAIN])
        m["xmain"] = np.ascontiguousarray(x_prompt[b, half * NMAIN:(half + 1) * NMAIN])
        m["xsamp"] = np.ascontiguousarray(x_sample[4 * c:4 * c + 4].reshape(NSAMP, D))
        m["flag"] = np.full((128, 1), float(half), np.float32)
        low = np.where(np.arange(128) < 64, -30000.0, 0.0)
        high = np.where(np.arange(128) >= 64, -30000.0, 0.0)
        halo = np.full(128, 0.0 if half else -30000.0)
        m["hbias"] = np.ascontiguousarray(np.stack([halo, np.minimum(halo, low), low, high], axis=1).astype(np.float32))
        m["ck"] = np.ascontiguousarray(f(cache_win_k)[0, 4 * c:4 * c + 4].reshape(4, 128, 128))
        m["cv"] = np.ascontiguousarray(f(cache_win_v)[0, 4 * c:4 * c + 4].reshape(4, 128, 128))
        m["st_re"] = np.ascontiguousarray(np.moveaxis(
            np.stack([st_layout(f(state_ssm_re)[0, 4 * c + s]) for s in range(4)]), 0, 2))
        m["st_im"] = np.ascontiguousarray(np.moveaxis(
            np.stack([st_layout(f(state_ssm_im)[0, 4 * c + s]) for s in range(4)]), 0, 2))
        in_maps.append(m)

    return in_maps


def assemble(R):

    def st_unlayout(a):
        a = np.asarray(a).reshape(2, 64, 16)
        return np.ascontiguousarray(np.transpose(a, (2, 0, 1)).reshape(32, 64))

    y_p = np.stack([np.concatenate([R[2 * b]["y_main"], R[2 * b + 1]["y_main"]], axis=0) for b in range(4)])
    y_s = np.concatenate([R[c]["y_samp"].reshape(4, 64, D) for c in range(8)], axis=0)
    kp = np.stack([R[2 * b + 1]["kwin_p"].reshape(128, 2, 64) for b in range(4)])[None]
    vp = np.stack([R[2 * b + 1]["vwin_p"].reshape(128, 2, 64) for b in range(4)])[None]
    rp = np.stack([st_unlayout(R[2 * b + 1]["sre_p"]) for b in range(4)])[None]
    ip = np.stack([st_unlayout(R[2 * b + 1]["sim_p"]) for b in range(4)])[None]
    ks = np.concatenate([R[c]["kwin_s"].reshape(4, 128, 2, 64) for c in range(8)], axis=0)[None]
    vs = np.concatenate([R[c]["vwin_s"].reshape(4, 128, 2, 64) for c in range(8)], axis=0)[None]
    rs = np.concatenate([np.stack([st_unlayout(R[c]["sre_s"][:, :, s]) for s in range(4)]) for c in range(8)])[None]
    is_ = np.concatenate([np.stack([st_unlayout(R[c]["sim_s"][:, :, s]) for s in range(4)]) for c in range(8)])[None]
    outs = (y_p, y_s, kp, vp, rp, ip, ks, vs, rs, is_)
    return tuple(np.ascontiguousarray(o.astype(np.float32)) for o in outs)


def kernel(**inputs):
    in_maps = make_in_maps(**inputs)
    nc = build_nc()
    res = run_bass_kernel_spmd(nc, in_maps, core_ids=list(range(8)))
    return assemble(res.results)
```

```python
import math
from contextlib import ExitStack
import numpy as np
import concourse.bass as bass
import concourse.mybir as mybir
from concourse.bass_utils import run_bass_kernel_spmd

F32 = mybir.dt.float32
BF16 = mybir.dt.bfloat16
I32 = mybir.dt.int32
AF = mybir.ActivationFunctionType
ALU = mybir.AluOpType

D = 1024
KT = 8
NST = 256
T = 8
NCH = NST // T
DFF = 2816
NJ = 22
DIN = 1280
ALPHA = 2.0 ** 0.25
EPS = 1e-5
NMAIN = 2048
NSAMP = 256
NTOK = NMAIN + NSAMP
PI = math.pi
KLIST = list(range(9)) + [8 * c for c in range(NCH)] + list(range(7, -1, -1)) + [T * NCH]
NK = len(KLIST)
WARM = True
TOP_STACK = [None]
DETACHED = {}
COMPUTE = ("pe", "act", "dve", "pool")
QUEUES = ("sp",)


class Prog:
    ISSUE = {"sp": 60.0, "pool": 900.0, "act": 150.0}

    def __init__(self, nc, tag, pers):
        self.nc = nc
        self.tag = tag
        self.pers = pers
        self.es = ExitStack()
        self.all = []
        self.detached_names = set()
        self.prio_bias = 0
        self.pe_ghz = 1.25
        self.filler = None
        self.fill_dur = 230.0
        self.fill_gap = 900.0
        self.pre_waits = []
        self.lastw = {}
        self.readers = {}
        self.sems = {}

    def sb(self, name, shape, dt):
        return self.es.enter_context(self.nc.sbuf_tensor(self.tag + name, list(shape), dt))

    def ps(self, name, shape, dt=F32):
        return self.es.enter_context(self.nc.psum_tensor(self.tag + name, list(shape), dt))

    def _record(self, eng, fn, reads, writes, dur, dma, lat):
        deps = set()
        for k in reads:
            ev = self.lastw.get(k)
            if ev is not None:
                deps.add(ev)
        for k in writes:
            ev = self.lastw.get(k)
            if ev is not None:
                deps.add(ev)
            deps.update(self.readers.get(k, ()))
        oid = len(self.all)
        self.all.append(dict(id=oid, eng=eng, fn=fn, deps=deps, dur=dur, dma=dma, lat=lat, prio=oid + self.prio_bias))
        for k in reads:
            self.readers.setdefault(k, []).append(oid)
        for k in writes:
            self.lastw[k] = oid
            self.readers[k] = []
        return oid

    def op(self, eng, fn, reads=(), writes=(), dur=100.0):
        return self._record(eng, fn, reads, writes, dur, None, 0.0)

    def dma(self, q, sem, fn, reads=(), writes=(), final=False, nbytes=65536, detached=False):
        if detached:
            self.detached_names.add(sem)
        return self._record(q, fn, reads, writes, self.ISSUE.get(q, 100.0), sem, 2000.0 + nbytes / 120.0)

    def _sem(self, key):
        if key not in self.sems:
            name = self.tag + "s_" + "_".join(str(x) for x in key)
            stack = TOP_STACK[0] if (key[0] == 'd' and key[1] in self.detached_names) else self.pers
            self.sems[key] = stack.enter_context(self.nc.semaphore(name))
        return self.sems[key]

    def _schedule(self):
        import heapq
        ops = self.all
        n = len(ops)
        succ = [[] for _ in range(n)]
        indeg = [0] * n
        for o in ops:
            indeg[o["id"]] = len(o["deps"])
            for d in o["deps"]:
                succ[d].append(o["id"])
        engs = COMPUTE + QUEUES
        ready = {e: [] for e in engs}
        ready_t = [0.0] * n
        done_t = [0.0] * n
        free_at = {e: 0.0 for e in engs}
        order = {e: [] for e in engs}
        for o in ops:
            if indeg[o["id"]] == 0:
                heapq.heappush(ready[o["eng"]], o["id"])
        remaining = n
        while remaining:
            best = None
            for e in engs:
                if not ready[e]:
                    continue
                t_free = free_at[e]
                cand = [i for i in ready[e] if ready_t[i] <= t_free]
                if cand:
                    i = min(cand, key=lambda j: (ops[j]["prio"], j))
                    t = t_free
                else:
                    i = min(ready[e], key=lambda j: (ready_t[j], ops[j]["prio"], j))
                    t = ready_t[i]
                if best is None or t < best[0]:
                    best = (t, e, i)
            t, e, i = best
            if e == "pe" and self.filler is not None and t - free_at[e] > self.fill_gap:
                nf = int((t - free_at[e] - 200.0) // self.fill_dur)
                order[e].extend([-1] * max(0, nf))
            ready[e].remove(i)
            heapq.heapify(ready[e])
            o = ops[i]
            order[e].append(i)
            end = t + o["dur"]
            free_at[e] = end
            done_t[i] = end + o["lat"]
            remaining -= 1
            for j in succ[i]:
                indeg[j] -= 1
                ready_t[j] = max(ready_t[j], done_t[i])
                if indeg[j] == 0:
                    heapq.heappush(ready[ops[j]["eng"]], j)
        self.est_ns = max(done_t) if n else 0.0
        return order

    def emit(self):
        nc = self.nc
        ops = self.all
        order = self._schedule()
        pos = {}
        for e, lst in order.items():
            for p, i in enumerate(lst):
                if i >= 0:
                    pos[i] = p
        dma_val = {}
        dma_cnt = {}
        for e in COMPUTE + QUEUES:
            for i in order[e]:
                if i < 0:
                    continue
                sname = ops[i]["dma"]
                if sname is not None:
                    dma_cnt[sname] = dma_cnt.get(sname, 0) + 1
                    dma_val[i] = 16 * dma_cnt[sname]
        waits = {}
        milestone = {e: set() for e in COMPUTE}
        for e in COMPUTE + QUEUES:
            known = {}
            for i in order[e]:
                if i < 0:
                    continue
                w = {}
                for d in ops[i]["deps"]:
                    od = ops[d]
                    if od["dma"] is not None:
                        key, val = ('d', od["dma"]), dma_val[d]
                    else:
                        if od["eng"] == e and e == "pe":
                            continue
                        key, val = ('c', od["eng"]), pos[d]
                    if known.get(key, -1) >= val:
                        continue
                    if w.get(key, -1) < val:
                        w[key] = val
                for key, val in w.items():
                    known[key] = val
                    if key[0] == 'c':
                        milestone[key[1]].add(val)
                waits[i] = list(w.items())
        mval = {}
        for e in COMPUTE:
            ms = sorted(milestone[e])
            mval[e] = {p: k + 1 for k, p in enumerate(ms)}
            if ms:
                self._sem(('c', e))
        for sname in dma_cnt:
            self._sem(('d', sname))

        def run_engine(e, eng):
            if e == "sp":
                for (h, v) in self.pre_waits:
                    eng.wait_ge(h, v)
            for p, i in enumerate(order[e]):
                if i < 0:
                    self.filler(eng)
                    continue
                o = ops[i]
                for (key, val) in waits[i]:
                    if key[0] == 'c':
                        eng.wait_ge(self._sem(key), mval[key[1]][val])
                    else:
                        eng.wait_ge(self._sem(key), val)
                ins = o["fn"](eng)
                if o["dma"] is not None:
                    ins.then_inc(self._sem(('d', o["dma"])), 16)
                elif e in COMPUTE and p in mval[e]:
                    ins.then_inc(self._sem(('c', e)), 1)
            if e == "sp":
                for a, cnt in dma_cnt.items():
                    if a in self.detached_names:
                        DETACHED[a] = (self._sem(('d', a)), 16 * cnt)
                    else:
                        eng.wait_ge(self._sem(('d', a)), 16 * cnt)

        with nc.Block() as block:
            @block.sync
            def _(eng):
                run_engine("sp", eng)

            @block.tensor
            def _(eng):
                run_engine("pe", eng)

            @block.scalar
            def _(eng):
                run_engine("act", eng)

            @block.vector
            def _(eng):
                run_engine("dve", eng)

            @block.gpsimd
            def _(eng):
                run_engine("pool", eng)
        self.es.close()

    @staticmethod
    def _fs(ap):
        n = 1
        for d in ap.shape[1:]:
            n *= d
        return n

    def mm(self, out, lhsT, rhs, start, stop, reads, writes, tp=None):
        if getattr(writes, "grp", None) is not None:
            writes = list(writes) + [writes.grp]
        dur = max(64, self._fs(rhs)) / self.pe_ghz + 12.0
        if tp is None:
            return self.op("pe", lambda e: e.matmul(out, lhsT=lhsT, rhs=rhs, start=start, stop=stop), reads, writes, dur)
        return self.op("pe", lambda e: e.matmul(out, lhsT=lhsT, rhs=rhs, start=start, stop=stop,
                                                tile_position=tp, skip_group_check=True), reads, writes, dur)

    def tr(self, out, in_, ident, reads, writes):
        return self.op("pe", lambda e: e.transpose(out=out, in_=in_, identity=ident), reads, writes, 110.0)

    def act(self, out, in_, func, reads, writes, scale=1.0, bias=0.0):
        extra = 0.0 if func in (AF.Identity, AF.Copy) else 500.0
        return self.op("act", lambda e: e.activation(out=out, in_=in_, func=func, bias=bias, scale=scale),
                       reads, writes, 230.0 + extra + self._fs(out) / 1.2)

    def _edur(self, eng, out):
        n = self._fs(out)
        if eng == "dve":
            return 120.0 + n * (2.6 if n >= 256 else 1.1)
        if eng == "pool":
            return 250.0 + n * 2.4
        return 230.0 + n / 1.2

    def tt(self, eng, out, in0, in1, op, reads, writes):
        return self.op(eng, lambda e: e.tensor_tensor(out=out, in0=in0, in1=in1, op=op), reads, writes,
                       self._edur(eng, out))

    def ts(self, eng, out, in0, s1, s2, op0, op1, reads, writes):
        if op1 is None:
            return self.op(eng, lambda e: e.tensor_scalar(out=out, in0=in0, scalar1=s1, scalar2=None, op0=op0),
                           reads, writes, self._edur(eng, out))
        return self.op(eng, lambda e: e.tensor_scalar(out=out, in0=in0, scalar1=s1, scalar2=s2, op0=op0, op1=op1),
                       reads, writes, self._edur(eng, out))

    def stt(self, out, in0, scalar, in1, op0, op1, reads, writes):
        return self.op("dve", lambda e: e.scalar_tensor_tensor(out=out, in0=in0, scalar=scalar, in1=in1,
                                                               op0=op0, op1=op1), reads, writes,
                       120.0 + 2.6 * self._fs(out))

    def cp(self, eng, out, in_, reads, writes):
        if eng == "act":
            return self.op("act", lambda e: e.copy(out=out, in_=in_), reads, writes, self._edur("act", out))
        return self.op(eng, lambda e: e.tensor_copy(out=out, in_=in_), reads, writes, self._edur(eng, out))


class _Stop(Exception):
    pass


def build_nc(stop=None):
    nc = bass.Bass("TRN2", target_bir_lowering=False)
    TOP_STACK[0] = ExitStack()
    DETACHED.clear()
    try:
        _build(nc, stop)
    except _Stop:
        pass
    TOP_STACK[0].close()
    return nc


def _build(nc, stop):

    def din(name, shape, dt=F32):
        return nc.dram_tensor(name, list(shape), dt, kind="ExternalInput").ap()

    def dout(name, shape, dt=F32):
        return nc.dram_tensor(name, list(shape), dt, kind="ExternalOutput").ap()

    xprev = din("xprev", [NMAIN, D])
    xmain = din("xmain", [NMAIN, D])
    xsamp = din("xsamp", [NSAMP, D])
    flag_d = din("flag", [128, 1])
    hbias_d = din("hbias", [128, 4])
    ident_d = din("ident", [128, 128])
    emask_d = din("emask", [128, 2])
    gbc_d = din("gbc", [6, 128, D])
    gcol_d = din("gcol", [128, 6, KT])
    kvec_d = din("kvec", [128, NK])
    lamr_d = din("lamr", [128, 16])
    lami_d = din("lami", [128, 16])
    lstep_d = din("lstep", [128, 16])
    bqr_d = din("bqr", [128, 16, 16])
    bqi_d = din("bqi", [128, 16, 16])
    cqr_d = din("cqr", [128, 16, 16])
    cqi_d = din("cqi", [128, 16, 16])
    dcol_d = din("dcol", [128, 4])
    bglu_d = din("bglucol", [128, 4])
    sink_d = din("sinkbc", [128, 8])
    ck_d = din("ck", [4, 128, 128])
    cv_d = din("cv", [4, 128, 128])
    str_d = din("st_re", [128, 16, 4])
    sti_d = din("st_im", [128, 16, 4])
    w_in_d = din("w_in", [D, DIN])
    w_glu_d = din("w_glu", [512, 512])
    w_out_d = din("w_out", [D, D])
    w_gu_d = din("w_gate_up", [D, 2 * DFF])
    w_dn_d = din("w_down", [DFF, D])

    y_main = dout("y_main", [NMAIN, D])
    y_samp = dout("y_samp", [NSAMP, D])
    kwin_p = dout("kwin_p", [128, 128])
    vwin_p = dout("vwin_p", [128, 128])
    sre_p = dout("sre_p", [128, 16])
    sim_p = dout("sim_p", [128, 16])
    kwin_s = dout("kwin_s", [4, 128, 128])
    vwin_s = dout("vwin_s", [4, 128, 128])
    sre_s = dout("sre_s", [128, 16, 4])
    sim_s = dout("sim_s", [128, 16, 4])

    wgu_nat = nc.dram_tensor("wgu_nat", [D, 2 * DFF], BF16).ap()
    wdn_sc = nc.dram_tensor("wdn_sc", [DFF, D], BF16).ap()
    hT_sc = nc.dram_tensor("hT_sc", [128, KT, NTOK], BF16).ap()
    r1_sc = nc.dram_tensor("r1_sc", [NTOK, D], F32).ap()

    ES = ExitStack()

    def pers(name, shape, dt):
        return ES.enter_context(nc.sbuf_tensor("p_" + name, list(shape), dt))

    identb = pers("identb", [128, 128], BF16)
    emask = pers("emask", [128, 2], F32)
    flag = pers("flag", [128, 1], F32)
    hbias = pers("hbias", [128, 4], F32)
    zbias = pers("zbias", [128, 1], F32)
    mhalf = pers("mhalf", [128, 1], F32)
    cfm = pers("cfm", [128, 10], F32)
    ckd = pers("ckd", [128, 4], F32)
    ckvb = pers("ckvb", [128, 256], F32)
    gcol = pers("gcol", [128, 6, KT], F32)
    G0a = pers("G0a", [128, D], F32)
    B0a = pers("B0a", [128, D], F32)
    dcol = pers("dcol", [128, 4], F32)
    bglu = pers("bglu", [128, 4], F32)
    sinkexp = pers("sinkexp", [128, 8], F32)
    kvec = pers("kvec", [128, NK], F32)
    w_inb = pers("w_inb", [128, KT, DIN], BF16)
    w_kd = pers("w_kd", [128, KT, 128], BF16)
    w_outb = pers("w_outb", [128, KT, D], BF16)
    w_glub = pers("w_glub", [128, 4, 512], BF16)
    cosMt = pers("cosMt", [128, 16, NCH], F32)
    sinMt = pers("sinMt", [128, 16, NCH], F32)
    aLr_t = pers("aLr_t", [128, 16], F32)
    aLi_t = pers("aLi_t", [128, 16], F32)
    PcR_t = pers("PcR_t", [128, 16, NCH], F32)
    PcI_t = pers("PcI_t", [128, 16, NCH], F32)
    PvR_t = pers("PvR_t", [128, 16, NCH], F32)
    PvI_t = pers("PvI_t", [128, 16, NCH], F32)
    Wst = pers("Wst", [128, 4, 8, 2, 128], BF16)
    Cmod = pers("Cmod", [128, 16, 8, 2, 32], BF16)
    Kblk = pers("Kblk", [128, 4, 8, 128], BF16)
    rhot = pers("rhot", [128, 16, NCH], F32)
    st = dict(tp=0, gb=0)
    P = Prog(nc, "a_", ES)
    tpb = [P.ps("tp%d" % i, [128, 1024], BF16) for i in range(2)]
    gb = [P.ps("gb%d" % i, [128, 512], F32) for i in range(6)]

    class KeyList(list):
        grp = None

    def _kl(keys, grp=None):
        k = KeyList(keys)
        k.grp = grp
        return k

    def next_tp(kind="F"):
        if kind == "F":
            return tpb[0], _kl([("tp", 0)])
        if kind == "L":
            return tpb[1], _kl([("tp", 1, q) for q in range(4)])
        q = st["tp"] % 4
        st["tp"] += 1
        return tpb[1][:, q * 256:(q + 1) * 256], _kl([("tp", 1, q)])

    def next_gb(pool="M"):
        grp = {"F": "F", "S": "S", "Sh": "S", "Q": "M", "M": "M"}[pool]
        c = st.setdefault(grp, 0)
        st[grp] = c + 1
        b = {"F": 0, "S": 2, "M": 4}[grp] + c % 2
        if grp == "F" and st.get("warm"):
            b = 0
        return gb[b], _kl([("gb", b)])

    identf = P.sb("identf", [128, 128], F32)

    ldc = dict(n=0)

    def ld(sem, dst, src, key):
        ldc["n"] += 1
        return P.dma("sp", "u%d" % ldc["n"], lambda e: e.dma_start(out=dst, in_=src), writes=[key])

    ld("c0", identf[:], ident_d, "identf")
    ld("c0", emask[:], emask_d, "emask")
    ld("c0", flag[:], flag_d, "flag")
    ld("c0", hbias[:], hbias_d, "hbias")
    ld("c0", gcol[:], gcol_d, "gcol")
    ld("c0", dcol[:], dcol_d, "dcol")
    ld("c0", bglu[:], bglu_d, "bglu")
    P.op("act", lambda e: e.mul(out=bglu[:], in_=bglu[:], mul=0.5), ["bglu"], ["bglu"])
    ld("c0", sinkexp[:], sink_d, "sinkexp")
    ld("c0", kvec[:], kvec_d, "kvec")
    P.cp("dve", identb[:], identf[:], ["identf"], ["identb"])
    P.op("pool", lambda e: e.memset(zbias[:], 0.0), writes=["zbias"])
    P.op("pool", lambda e: e.memset(mhalf[:], -0.5), writes=["mhalf"])
    P.act(sinkexp[:], sinkexp[:], AF.Exp, ["sinkexp"], ["sinkexp"])
    for i, tbl in enumerate((G0a, B0a)):
        ld("c1", tbl[:], gbc_d[i], ("tbl", i))
        P.op("act", lambda e, tbl=tbl: e.mul(out=tbl[:], in_=tbl[:], mul=ALPHA), [("tbl", i)], [("tbl", i)])

    stg = [P.sb("stg%d" % i, [128, DIN], F32) for i in range(3)]
    cast_engs = ["dve", "pool", "act"]
    sc = dict(n=0)

    def stage_cast(src_ap, ncols, dst_ap, mul=None):
        i = sc["n"] % 3
        eng = cast_engs[sc["n"] % 3]
        sc["n"] += 1
        P.dma("sp", "stg%d" % i, lambda e: e.dma_start(out=stg[i][:, 0:ncols], in_=src_ap), writes=[("stg", i)],
              nbytes=128 * ncols * 4)
        if mul is None:
            P.cp(eng, dst_ap, stg[i][:, 0:ncols], [("stg", i)], ["wres"])
        else:
            P.op("act", lambda e: e.mul(out=dst_ap, in_=stg[i][:, 0:ncols], mul=mul), [("stg", i)], ["wres"],
                 230.0 + ncols / 1.2)

    onesr = P.sb("onesr", [1, 128], F32)
    one1 = P.sb("one1", [1, 1], F32)
    crow = P.sb("crow", [1, DIN], F32)
    crkd = P.sb("crkd", [1, 128], F32)
    P.op("pool", lambda e: e.memset(onesr[:], 1.0), writes=["onesr"])
    P.op("pool", lambda e: e.memset(one1[:], 1.0), writes=["one1"])
    P.op("pool", lambda e: e.memset(crkd[:], 0.0), writes=["crkd"])
    cbanks = [next_gb("F"), next_gb("S"), next_gb("M")]
    for kt in range(KT):
        i = sc["n"] % 3
        sc["n"] += 1
        P.dma("sp", "stg%d" % i, lambda e, i=i, kt=kt: e.dma_start(out=stg[i][:, 0:DIN], in_=w_in_d[kt * 128:(kt + 1) * 128, :]),
              writes=[("stg", i)], nbytes=128 * DIN * 4)
        P.act(w_inb[:, kt, :], stg[i][:, 0:DIN], AF.Identity, [("stg", i), "gcol"], ["wres"], scale=gcol[:, 0, kt:kt + 1])
        for cb, (c0, cn) in enumerate(((0, 512), (512, 512), (1024, 256))):
            bank, bk = cbanks[cb]
            P.mm(bank[0:1, 0:cn], gcol[:, 1, kt:kt + 1], stg[i][:, c0:c0 + cn], kt == 0, kt == KT - 1,
                 [("stg", i), "gcol"], bk)
    for cb, (c0, cn) in enumerate(((0, 512), (512, 512), (1024, 256))):
        bank, bk = cbanks[cb]
        P.cp("dve", crow[:, c0:c0 + cn], bank[0:1, 0:cn], bk, ["crow"])
    P.cp("dve", crkd[:, 0:64], crow[:, 576:640], ["crow", "crkd"], ["crkd"])
    P.cp("dve", crkd[:, 64:128], crow[:, 512:576], ["crow", "crkd"], ["crkd"])
    bank, bk = next_gb("F")
    for ot in range(10):
        P.mm(bank[:, ot:ot + 1], crow[:, ot * 128:(ot + 1) * 128], one1[:], True, True, ["crow", "one1"], bk)
    P.mm(bank[:, 16:17], crkd[:], one1[:], True, True, ["crkd", "one1"], bk)
    P.cp("dve", cfm[:], bank[:, 0:10], bk, ["cfm"])
    P.cp("dve", ckd[:, 0:1], bank[:, 16:17], bk, ["ckd"])
    bank, bk = next_gb("S")
    P.mm(bank[:, 0:256], onesr[:], crow[:, 512:768], True, True, ["crow", "onesr"], bk)
    P.cp("dve", ckvb[:], bank[:, 0:256], bk, ["ckvb"])
    for kt in range(KT):
        stage_cast(w_out_d[kt * 128:(kt + 1) * 128, :], D, w_outb[:, kt, :], 0.5 if kt >= 4 else None)
    for kt in range(4):
        stage_cast(w_glu_d[kt * 128:(kt + 1) * 128, :], 512, w_glub[:, kt, :])
    P.cp("pool", w_kd[:, :, 0:64], w_inb[:, :, 576:640], ["wres"], ["wres2"])
    P.cp("pool", w_kd[:, :, 64:128], w_inb[:, :, 512:576], ["wres"], ["wres2"])
    def stop_here(name):
        if stop == name:
            P.emit()
            raise _Stop()

    def new_prog(tag):
        stop_here({"c_": "b"}[tag])
        P.emit()
        nc.all_engine_barrier()
        st["tp"] = 0
        st["gb"] = 0
        return Prog(nc, tag, ES)

    lamr = P.sb("lamr", [128, 16], F32)
    lami = P.sb("lami", [128, 16], F32)
    dtt = P.sb("dtt", [128, 16], F32)
    lam = P.sb("lam", [128, 16], F32)
    th = P.sb("th", [128, 16], F32)
    ang = P.sb("ang", [128, 16, NK], F32)
    angc = P.sb("angc", [128, 16, NK], F32)
    tmpf = P.sb("tmpf", [128, 16, NK], F32)
    tmpi = P.sb("tmpi", [128, 16, NK], I32)
    msk = P.sb("msk", [128, 16, NK], F32)
    mag = P.sb("mag", [128, 16, NK], F32)
    ld("c2", lamr[:], lamr_d, "lamr")
    ld("c2", lami[:], lami_d, "lami")
    ld("c2", dtt[:], lstep_d, "dtt")
    P.act(dtt[:], dtt[:], AF.Exp, ["dtt"], ["dtt"])
    P.tt("dve", lam[:], lamr[:], dtt[:], ALU.mult, ["lamr", "dtt"], ["lam"])
    P.tt("dve", th[:], lami[:], dtt[:], ALU.mult, ["lami", "dtt"], ["th"])
    kv_b = kvec[:].unsqueeze(1).broadcast_to([128, 16, NK])
    P.tt("dve", ang[:], th[:].unsqueeze(2).broadcast_to([128, 16, NK]), kv_b, ALU.mult, ["th", "kvec"], ["ang"])
    P.tt("dve", mag[:], lam[:].unsqueeze(2).broadcast_to([128, 16, NK]), kv_b, ALU.mult, ["lam", "kvec"], ["mag"])
    P.act(mag[:], mag[:], AF.Exp, ["mag"], ["mag"])
    P.ts("dve", angc[:], ang[:], PI / 2, None, ALU.add, None, ["ang"], ["angc"])

    def sin_reduced(dst, src, key_src, key_dst):
        P.ts("dve", tmpf[:], src[:], 1.0 / (2 * PI), None, ALU.mult, None, [key_src], ["tmpf"])
        P.cp("dve", tmpi[:], tmpf[:], ["tmpf"], ["tmpi"])
        P.cp("dve", tmpf[:], tmpi[:], ["tmpi"], ["tmpf"])
        P.stt(src[:], tmpf[:], -2 * PI, src[:], ALU.mult, ALU.add, ["tmpf", key_src], [key_src])
        P.ts("dve", msk[:], src[:], PI, None, ALU.is_gt, None, [key_src], ["msk"])
        P.stt(src[:], msk[:], -2 * PI, src[:], ALU.mult, ALU.add, ["msk", key_src], [key_src])
        P.ts("dve", msk[:], src[:], -PI, None, ALU.is_lt, None, [key_src], ["msk"])
        P.stt(src[:], msk[:], 2 * PI, src[:], ALU.mult, ALU.add, ["msk", key_src], [key_src])
        P.ts("dve", src[:], src[:], 3.1415925, -3.1415925, ALU.min, ALU.max, [key_src], [key_src])
        P.act(dst[:], src[:], AF.Sin, [key_src], [key_dst])

    sin_reduced(ang, ang, "ang", "ang")
    sin_reduced(angc, angc, "angc", "angc")
    P.cp("dve", cosMt[:], angc[:, :, 9:9 + NCH], ["angc"], ["cosMt"])
    P.cp("dve", sinMt[:], ang[:, :, 9:9 + NCH], ["ang"], ["sinMt"])
    P.tt("dve", angc[:], mag[:], angc[:], ALU.mult, ["mag", "angc"], ["angc"])
    P.tt("dve", ang[:], mag[:], ang[:], ALU.mult, ["mag", "ang"], ["ang"])

    s16 = [P.sb("s16_%d" % i, [128, 16], F32) for i in range(8)]
    nr, den, fre, fim, t0, t1 = s16[0], s16[1], s16[2], s16[3], s16[4], s16[5]
    ar = angc[:, :, 1]
    ai = ang[:, :, 1]
    P.ts("dve", nr[:], ar, -1.0, None, ALU.add, None, ["angc"], ["nr"])
    P.tt("dve", den[:], lamr[:], lamr[:], ALU.mult, ["lamr"], ["den"])
    P.tt("dve", t0[:], lami[:], lami[:], ALU.mult, ["lami"], ["t0"])
    P.tt("dve", den[:], den[:], t0[:], ALU.add, ["den", "t0"], ["den"])
    P.op("dve", lambda e: e.reciprocal(out=den[:], in_=den[:]), ["den"], ["den"])
    P.tt("dve", t0[:], nr[:], lamr[:], ALU.mult, ["nr", "lamr"], ["t0"])
    P.tt("dve", t1[:], ai, lami[:], ALU.mult, ["ang", "lami"], ["t1"])
    P.tt("dve", t0[:], t0[:], t1[:], ALU.add, ["t0", "t1"], ["t0"])
    P.tt("dve", fre[:], t0[:], den[:], ALU.mult, ["t0", "den"], ["fre"])
    P.tt("dve", t0[:], ai, lamr[:], ALU.mult, ["ang", "lamr"], ["t0"])
    P.tt("dve", t1[:], nr[:], lami[:], ALU.mult, ["nr", "lami"], ["t1"])
    P.tt("dve", t0[:], t0[:], t1[:], ALU.subtract, ["t0", "t1"], ["t0"])
    P.tt("dve", fim[:], t0[:], den[:], ALU.mult, ["t0", "den"], ["fim"])

    Bqr = P.sb("Bqr", [128, 16, 16], F32)
    Bqi = P.sb("Bqi", [128, 16, 16], F32)
    Cqr = P.sb("Cqr", [128, 16, 16], F32)
    Cqi = P.sb("Cqi", [128, 16, 16], F32)
    Bbr = P.sb("Bbr", [128, 16, 16], F32)
    Bbi = P.sb("Bbi", [128, 16, 16], F32)
    u0 = P.sb("u0", [128, 16, 16], F32)
    u1 = P.sb("u1", [128, 16, 16], F32)
    ld("c2", Bqr[:], bqr_d, "Bqr")
    ld("c2", Bqi[:], bqi_d, "Bqi")
    ld("c2", Cqr[:], cqr_d, "Cqr")
    ld("c2", Cqi[:], cqi_d, "Cqi")
    fre_b = fre[:].unsqueeze(2).broadcast_to([128, 16, 16])
    fim_b = fim[:].unsqueeze(2).broadcast_to([128, 16, 16])
    P.tt("dve", u0[:], Bqr[:], fre_b, ALU.mult, ["Bqr", "fre"], ["u0"])
    P.tt("dve", u1[:], Bqi[:], fim_b, ALU.mult, ["Bqi", "fim"], ["u1"])
    P.tt("dve", Bbr[:], u0[:], u1[:], ALU.subtract, ["u0", "u1"], ["Bbr"])
    P.tt("dve", u0[:], Bqi[:], fre_b, ALU.mult, ["Bqi", "fre"], ["u0"])
    P.tt("dve", u1[:], Bqr[:], fim_b, ALU.mult, ["Bqr", "fim"], ["u1"])
    P.tt("dve", Bbi[:], u0[:], u1[:], ALU.add, ["u0", "u1"], ["Bbi"])

    def cprod(dst_r, dst_i, xr, xi, pidx0, n, kr, ki):
        pr = angc[:, :, pidx0:pidx0 + n].unsqueeze(3).broadcast_to([128, 16, n, 16])
        pi = ang[:, :, pidx0:pidx0 + n].unsqueeze(3).broadcast_to([128, 16, n, 16])
        xrb = xr[:].unsqueeze(2).broadcast_to([128, 16, n, 16])
        xib = xi[:].unsqueeze(2).broadcast_to([128, 16, n, 16])
        P.tt("dve", dst_r, xrb, pr, ALU.mult, [kr, "angc"], ["cp_r"])
        P.tt("dve", w4[:, :, 0:n, :], xib, pi, ALU.mult, [ki, "ang"], ["w4"])
        P.tt("dve", dst_r, dst_r, w4[:, :, 0:n, :], ALU.subtract, ["cp_r", "w4"], ["cp_r"])
        P.tt("dve", dst_i, xrb, pi, ALU.mult, [kr, "ang"], ["cp_i"])
        P.tt("dve", w4[:, :, 0:n, :], xib, pr, ALU.mult, [ki, "angc"], ["w4"])
        P.tt("dve", dst_i, dst_i, w4[:, :, 0:n, :], ALU.add, ["cp_i", "w4"], ["cp_i"])

    w4 = P.sb("w4", [128, 16, 9, 16], F32)
    WBr = P.sb("WBr", [128, 16, 9, 16], F32)
    WBi = P.sb("WBi", [128, 16, 9, 16], F32)
    Xb = [P.sb("Xb%d" % i, [128, 4, 2, 16], BF16) for i in range(2)]

    cprod(WBr[:, :, 0:8, :], WBi[:, :, 0:8, :], Bbr, Bbi, 41, 8, "Bbr", "Bbi")
    n_x = 0
    for ct in range(4):
        for r in range(8):
            for comp, WB in enumerate((WBr, WBi)):
                xb = Xb[n_x % 2]
                xk = ("Xb", n_x % 2)
                n_x += 1
                P.tt("dve", xb[:], WB[:, 4 * ct:4 * ct + 4, r, :].unsqueeze(2).broadcast_to([128, 4, 2, 16]),
                     emask[:].unsqueeze(1).unsqueeze(3).broadcast_to([128, 4, 2, 16]), ALU.mult,
                     ["cp_r", "cp_i", "emask"], [xk])
                tp, tk = next_tp("F" if n_x % 2 else "L")
                P.tr(tp[:, 0:128], xb[:].rearrange("p a b c -> p (a b c)"), identb[:], [xk, "identb"], tk)
                P.cp("act", Wst[:, ct, r, comp, :], tp[:, 0:128], tk, ["Wst"])
    cprod(WBr[:], WBi[:], Cqr, Cqi, 0, 9, "Cqr", "Cqi")
    for comp, (CA, sgn) in enumerate(((WBr, 1.0), (WBi, -1.0))):
        for e2 in range(2):
            P.ts("dve", Cmod[:, :, :, comp, e2 * 16:(e2 + 1) * 16], CA[:, :, 1:9, :], emask[:, e2:e2 + 1], sgn,
                 ALU.mult, ALU.mult, ["cp_r", "cp_i", "emask"], ["Cmod"])
    Bexp = P.sb("Bexp", [128, 16, 2, 128], BF16)
    CAexp = P.sb("CAexp", [128, 4, 2, 4, 128], BF16)
    P.op("pool", lambda e: e.memset(Bexp[:], 0.0), writes=["Bexp"])
    P.op("pool", lambda e: e.memset(CAexp[:], 0.0), writes=["CAexp"])
    for jj in range(4):
        for comp, Bb in enumerate((Bbr, Bbi)):
            for e2 in range(2):
                P.ts("dve", Bexp[:, jj::4, comp, jj * 32 + e2 * 16:jj * 32 + (e2 + 1) * 16], Bb[:, jj::4, :],
                     emask[:, e2:e2 + 1], None, ALU.mult, None, ["Bbr", "Bbi", "emask", "Bexp"], ["Bexp"])
    for ct in range(4):
        for h in range(2):
            for jj in range(4):
                for comp, (CA, sgn) in enumerate(((WBr, 1.0), (WBi, -1.0))):
                    for e2 in range(2):
                        P.ts("dve", CAexp[:, jj, comp, :, jj * 32 + e2 * 16:jj * 32 + (e2 + 1) * 16],
                             CA[:, 4 * ct + jj, 4 * h:4 * h + 4, :], emask[:, e2:e2 + 1], sgn, ALU.mult, ALU.mult,
                             ["cp_r", "cp_i", "emask", "CAexp"], ["CAexp"])
            bank, bk = next_gb("M")
            n = 0
            for jj in range(4):
                for comp in range(2):
                    P.mm(bank[:].rearrange("p (l c) -> p l c", c=128), Bexp[:, 4 * ct + jj, comp, :],
                         CAexp[:, jj, comp, :, :], n == 0, n == 7, ["Bexp", "CAexp"], bk)
                    n += 1
            P.cp("act", Kblk[:, ct, 4 * h:4 * h + 4, :], bank[:].rearrange("p (l c) -> p l c", c=128), bk, ["Kblk"])

    P.cp("dve", rhot[:], mag[:, :, 8:9].broadcast_to([128, 16, NCH]), ["mag"], ["rhot"])
    P.cp("dve", aLr_t[:], angc[:, :, NK - 1], ["angc"], ["aT"])
    P.cp("dve", aLi_t[:], ang[:, :, NK - 1], ["ang"], ["aT"])
    P.cp("dve", PcR_t[:], angc[:, :, 9:9 + NCH], ["angc"], ["PcR"])
    P.cp("dve", PcI_t[:], ang[:, :, 9:9 + NCH], ["ang"], ["PcI"])
    for c in range(NCH):
        P.cp("pool", PvR_t[:, :, c], angc[:, :, 9 + NCH - 1 - c], ["angc"], ["PvR"])
        P.cp("dve", PvI_t[:, :, c], ang[:, :, 9 + NCH - 1 - c], ["ang"], ["PvI"])
    P = new_prog("c_")
    tpb = [P.ps("tp%d" % i, [128, 1024], BF16) for i in range(2)]
    gb = [P.ps("gb%d" % i, [128, 512], F32) for i in range(6)]
    if WARM:
        P.pe_ghz = 2.0
        P.fill_dur = 80.0
        P.fill_gap = 300.0
        P.filler = lambda e: e.ldweights(identb[:])
    xt = [P.sb("xt%d" % i, [128, D], F32) for i in range(2)]
    xh16 = P.sb("xh16", [128, D], BF16)
    hh16 = P.sb("hh16", [128, D], BF16)
    r0s = [[P.sb("r0_%d" % i, [128, D], F32) for i in range(2)]] * 2
    lnsc = [dict(stats=P.sb("stats%d" % g, [128, 2, 6], F32), mv=P.sb("mv%d" % g, [128, 2], F32),
                 rstd=P.sb("rstd%d" % g, [128, 1], F32), nmr=P.sb("nmr%d" % g, [128, 1], F32)) for g in range(2)]
    x0T = P.sb("x0T", [128, KT, NST], BF16)
    qTs = [P.sb("qT%d" % p, [128, 4, NST], BF16) for p in range(2)]
    NW = 6
    kd = [P.sb("kd%d" % i, [128, NW * 128], BF16) for i in range(4)]
    vt = P.sb("vt", [128, NW, 2, 65], BF16)
    uPs = [P.sb("uP%d" % p, [128, 4, NCH, 2 * T - 1], BF16) for p in range(2)]
    catTs = [P.sb("catT", [128, KT, NST], BF16)] * 2
    cur = dict(par=0, w=[0, 0, 0], wn=0)

    def set_cur(par):
        cur.update(par=par, uP=uPs[par], qT=qTs[par], catT=catTs[par], r0=r0s[par])

    set_cur(0)
    S_r = P.sb("S_r", [128, 16, NCH], F32)
    S_i = P.sb("S_i", [128, 16, NCH], F32)
    M_r = P.sb("M_r", [128, 16, NCH], F32)
    M_i = P.sb("M_i", [128, 16, NCH], F32)
    G_r = P.sb("G_r", [128, 16, NCH], F32)
    G_i = P.sb("G_i", [128, 16, NCH], F32)
    H_r = P.sb("H_r", [128, 16, NCH], F32)
    H_i = P.sb("H_i", [128, 16, NCH], F32)
    red_r = P.sb("red_r", [128, 16], F32)
    red_i = P.sb("red_i", [128, 16], F32)
    Hprev = P.sb("Hprev", [128, 16, 2, NCH], BF16)
    Hc_r = P.sb("Hc_r", [128, 16], F32)
    Hc_i = P.sb("Hc_i", [128, 16], F32)
    Hs_r = P.sb("Hs_r", [128, 16, 4], F32)
    Hs_i = P.sb("Hs_i", [128, 16, 4], F32)
    Ho_r = P.sb("Ho_r", [128, 16, 4], F32)
    Ho_i = P.sb("Ho_i", [128, 16, 4], F32)
    inj_r = P.sb("inj_r", [128, 16, 4], F32)
    inj_i = P.sb("inj_i", [128, 16, 4], F32)
    it0 = P.sb("it0", [128, 16, 4], F32)
    it1 = P.sb("it1", [128, 16, 4], F32)
    y32 = P.sb("y32", [128, NST], F32)
    yg = P.sb("yg", [128, 4, NST], BF16)
    sg = P.sb("sg", [128, NST], BF16)
    PTa = [P.sb("PTa%d" % i, [128, 512], BF16) for i in range(2)]
    PTb = [P.sb("PTb%d" % i, [128, 512], BF16) for i in range(2)]
    den8 = P.sb("den8", [64, 8], F32)
    Atok = P.sb("Atok", [64, 8, 64], BF16)
    ckf = P.sb("ckf", [128, 128], F32)
    ckz = P.sb("ckz", [128, 4, 128], BF16)
    kc = [P.sb("kc%d" % i, [128, 4, 128], BF16) for i in range(4)]
    vc = P.sb("vc", [128, 4, 2, 65], BF16)
    kvo = P.sb("kvo", [128, 256], F32)
    z1s = [P.sb("z1_%d" % i, [128, D], F32) for i in range(2)]
    hTt = [P.sb("hTt%d" % i, [128, KT, 128], BF16) for i in range(2)]

    for kv in range(4):
        P.op("pool", lambda e, kv=kv: e.memset(kd[kv][:], 0.0), writes=[("kd", w) for w in range(NW)])
    P.op("pool", lambda e: e.memset(ckz[:], 0.0), writes=["ckz"])
    for p in range(2):
        P.op("pool", lambda e, p=p: e.memset(uPs[p][:], 0.0), writes=[("uT", p)])
    P.op("pool", lambda e: e.memset(vt[:], 1.0), writes=[("vt", w) for w in range(NW)])
    P.op("pool", lambda e: e.memset(vc[:], 1.0), writes=["vc"])
    P.op("pool", lambda e: e.memset(Hc_r[:], 0.0), writes=["Hin"])
    P.op("pool", lambda e: e.memset(Hc_i[:], 0.0), writes=["Hin"])
    cosM = cosMt[:]
    sinM = sinMt[:]

    ctr = dict(x=0, t=0)
    cast_pieces = []
    for r0_ in range(0, D, 32):
        cast_pieces.append((wgu_nat[r0_:r0_ + 32, :].rearrange("r (a c) -> r a c", c=1408),
                            w_gu_d[r0_:r0_ + 32, :].rearrange("r (a c) -> r a c", c=1408)))
    for r0_ in range(0, DFF, 128):
        cast_pieces.append((wdn_sc[r0_:r0_ + 128, :], w_dn_d[r0_:r0_ + 128, :]))
    NTILES = 2 * (NMAIN // NST) + 2 * (NTOK // NST)

    def release_casts(xkey):
        k = ctr.setdefault("cast", 0)
        t = ctr.setdefault("xtiles", 0)
        ctr["xtiles"] = t + 1
        left = max(1, NTILES - 4 - t)
        n = (len(cast_pieces) - k + left - 1) // left
        for kk in range(k, min(k + n, len(cast_pieces))):
            dst, src = cast_pieces[kk]
            P.dma("pool", "wcast%d" % (kk % 8), lambda e, dst=dst, src=src: e.dma_start(out=dst, in_=src),
                  reads=[xkey], writes=[("wcast", kk % 8)], detached=True, nbytes=720896)
        ctr["cast"] = min(k + n, len(cast_pieces))

    def ln_tile(src_ap, xbuf, xkey, resid, rkey, gi, bset=None):
        bset = gi if bset is None else bset
        stats, mv, rstd, nmr = (lnsc[bset][k] for k in ("stats", "mv", "rstd", "nmr"))
        kS, kM, kR, kN = (("ln", bset, k) for k in range(4))
        P.op("dve", lambda e: e.bn_stats(out=stats[:, 0, :], in_=src_ap[:, 0:512]), [xkey], [kS], 600.0)
        P.op("dve", lambda e: e.bn_stats(out=stats[:, 1, :], in_=src_ap[:, 512:1024]), [xkey], [kS], 600.0)
        P.op("dve", lambda e: e.bn_aggr(out=mv[:], in_=stats[:].rearrange("p a b -> p (a b)")), [kS], [kM])
        P.ts("dve", rstd[:], mv[:, 1:2], EPS, None, ALU.add, None, [kM], [kR])
        P.tt("pool", rstd[:], rstd[:], mhalf[:], ALU.pow, [kR, "mhalf"], [kR])
        P.stt(nmr[:], mv[:, 0:1], -1.0, rstd[:], ALU.mult, ALU.mult, [kM, kR], [kN])
        h16 = xh16 if bset == 0 else hh16
        P.act(h16[:], src_ap, AF.Identity, [xkey, kR, kN], [("h16", bset)], scale=rstd[:, 0:1], bias=nmr[:, 0:1])
        P.act(src_ap, src_ap, AF.Identity, [xkey, kR, kN], [xkey], scale=rstd[:, 0:1], bias=nmr[:, 0:1])
        if resid is not None:
            Gt, Bt = (G0a, B0a)
            P.tt("pool", resid[:], src_ap, Gt[:], ALU.mult, [xkey, ("tbl", 2 * gi)], [rkey])
            P.tt("pool", resid[:], resid[:], Bt[:], ALU.add, [rkey, ("tbl", 2 * gi + 1)], [rkey])

    def transpose_to(dstT, dkey, col0, gi, bset=None):
        bset = gi if bset is None else bset
        tp, tk = next_tp("F" if bset == 0 else "L")
        for kt in range(KT):
            h16 = xh16 if bset == 0 else hh16
            P.tr(tp[:, kt * 128:(kt + 1) * 128], h16[:, kt * 128:(kt + 1) * 128], identb[:], [("h16", bset), "identb"], tk)
        if gi == 1:
            P.cp("act", dstT[:].rearrange("p k t -> p (k t)"), tp[:, 0:KT * 128], tk, [dkey])
            return
        P.cp("act", dstT[:, :, col0:col0 + 128], tp[:, 0:KT * 128].rearrange("p (k t) -> p k t", t=128), tk, [dkey])
        return
        for kt in range(KT):
            eng = "act"
            if eng == "dve":
                P.ts("dve", dstT[:, kt, col0:col0 + 128], tp[:, kt * 128:(kt + 1) * 128], gcol[:, 2 * gi, kt:kt + 1],
                     gcol[:, 2 * gi + 1, kt:kt + 1], ALU.mult, ALU.add, tk + ["gcol"], [dkey])
            else:
                P.act(dstT[:, kt, col0:col0 + 128], tp[:, kt * 128:(kt + 1) * 128], AF.Identity, tk + ["gcol"], [dkey],
                      scale=gcol[:, 2 * gi, kt:kt + 1], bias=gcol[:, 2 * gi + 1, kt:kt + 1])

    def proj_fm(dst_ap, dkey, w_ap_fn, evac, chunked=False, bias=None):
        bank, bk = next_gb("F")
        for kt in range(KT):
            P.mm(bank[:, 0:NST], w_ap_fn(kt), x0T[:, kt, :], kt == 0, kt == KT - 1, ["x0T", "wres", "wres2"], bk)
        src = bank[:, 0:NST].rearrange("p (c r) -> p c r", r=T) if chunked else bank[:, 0:NST]
        if evac == "act":
            P.act(dst_ap, src, AF.Identity, bk + ["cfm"], [dkey], bias=bias)
        else:
            P.ts("dve", dst_ap, src, bias, None, ALU.add, None, bk + ["cfm"], [dkey])

    def front(xsrc, full, resid, halo_only=False):
        if full:
            wa, wb = cur["wn"] % NW, (cur["wn"] + 1) % NW
            cur["w"] = [(cur["wn"] - 1) % NW, wa, wb]
            cur["wn"] += 2
        for m in range(2):
            i = ctr["x"] % 2
            ctr["x"] += 1
            P.dma("sp", "xt%d" % i, lambda e, i=i, m=m: e.dma_start(out=xt[i][:], in_=xsrc[m * 128:(m + 1) * 128, :]),
                  writes=[("xt", i)], nbytes=128 * D * 4)
            release_casts(("xt", i))
            bs = m if not resid else 0
            ln_tile(xt[i][:], xt[i], ("xt", i), cur["r0"][m] if resid else None, ("r0", m), 0, bset=bs)
            transpose_to(x0T, "x0T", m * 128, 0, bset=bs)
        ev = ["act", "dve"]
        n = 0
        for ct in range(4):
            proj_fm(cur["uP"][:, ct, :, T - 1:2 * T - 1], ("uT", cur["par"]),
                    lambda kt, ct=ct: w_inb[:, kt, 768 + ct * 128:768 + (ct + 1) * 128], ev[n % 2], chunked=True,
                    bias=cfm[:, 6 + ct:7 + ct])
            n += 1
        if full:
            for t4 in range(0 if halo_only else 4):
                proj_fm(cur["qT"][:, t4, :], ("qT", cur["par"]), lambda kt, t4=t4: w_inb[:, kt, t4 * 128:(t4 + 1) * 128], ev[n % 2],
                        bias=cfm[:, t4:t4 + 1])
                n += 1
            for tile_b in range(2):
                bank, bk = next_gb("F")
                for kt in range(KT):
                    lw = w_kd[:, kt, :] if tile_b else w_inb[:, kt, 512:640]
                    P.mm(bank[:, 0:NST], lw, x0T[:, kt, :], kt == 0, kt == KT - 1, ["x0T", "wres", "wres2"], bk)
                bias = ckd[:, 0:1] if tile_b else cfm[:, 4:5]
                variants = ((2, 0, 64), (1, 64, 128)) if tile_b else ((0, 0, 64), (3, 64, 128))
                for (v, lo, hi) in variants:
                    for m in range(1 if halo_only else 0, 2):
                        wi = cur["w"][1 + m]
                        P.ts("dve", kd[v][lo:hi, wi * 128:(wi + 1) * 128], bank[lo:hi, m * 128:(m + 1) * 128],
                             bias[lo:hi, :], None, ALU.add, None, bk + ["ckd", "cfm"], [("kd", wi)])
                n += 1
            for m in range(1 if halo_only else 0, 2):
                bank, bk = next_gb("F")
                for kt in range(KT):
                    P.mm(bank[:, 0:128], x0T[:, kt, m * 128:(m + 1) * 128], w_inb[:, kt, 640:768], kt == 0, kt == KT - 1,
                         ["x0T", "wres"], bk)
                P.tt("dve", vt[:, cur["w"][1 + m], :, 0:64], bank[:, 0:128].rearrange("p (a b) -> p a b", b=64),
                     ckvb[:, 128:256].rearrange("p (a b) -> p a b", b=64), ALU.add, bk + ["ckvb"],
                     [("vt", cur["w"][1 + m])])
                n += 1

    def s_compute():
        banks = [next_gb("S"), next_gb("S"), next_gb("M"), next_gb("M")]
        for ct in range(4):
            for comp in range(2):
                col = (ct * 2 + comp) * NCH
                for r in range(T):
                    for jj in range(4):
                        bank, bk = banks[jj]
                        rhs = cur["uP"][jj * 32:(jj + 1) * 32, ct, :, T - 1 + r]
                        P.mm(bank[:, col:col + NCH], Wst[jj * 32:(jj + 1) * 32, ct, r, comp, :], rhs, r == 0, r == T - 1,
                             [("uT", cur["par"]), "Wst"], bk, tp=(jj * 32, 0))
        for jj in range(4):
            bank, bk = banks[jj]
            bv = bank[:, 0:256].rearrange("p (a b c) -> p a b c", a=4, b=2)
            P.cp("act", S_r[:, jj::4, :], bv[:, :, 0, :], bk, ["S_r"])
            P.cp("act", S_i[:, jj::4, :], bv[:, :, 1, :], bk, ["S_i"])

    def ssm_states(seq_starts, inj_fn):
        s_compute()
        ns = len(seq_starts)
        step = NCH // ns
        hr, hi = inj_fn()
        if ns > 1:
            aTr = PcR_t[:, :, 1:2].broadcast_to([128, 16, ns])
            aTi = PcI_t[:, :, 1:2].broadcast_to([128, 16, ns])
            cmul_tab(inj_r[:], inj_i[:], aTr, aTi, hr, hi, it0[:], it1[:], "ij")
            ir, ii = inj_r[:], inj_i[:]
        else:
            cmul_tab(inj_r[:, :, 0], inj_i[:, :, 0], PcR_t[:, :, 1], PcI_t[:, :, 1], hr, hi, it0[:, :, 0], it1[:, :, 0], "ij")
            ir, ii = inj_r[:, :, 0:1], inj_i[:, :, 0:1]
        P.tt("dve", S_r[:, :, ::step], S_r[:, :, ::step], ir, ALU.add, ["S_r", "ijr"], ["S_r"])
        P.tt("dve", S_i[:, :, ::step], S_i[:, :, ::step], ii, ALU.add, ["S_i", "iji"], ["S_i"])
        P.tt("dve", M_r[:], S_r[:], cosM, ALU.mult, ["S_r", "cosT"], ["M_r"])
        P.tt("pool", G_r[:], S_i[:], sinM, ALU.mult, ["S_i", "sinT"], ["G_r"])
        P.tt("dve", M_r[:], M_r[:], G_r[:], ALU.add, ["M_r", "G_r"], ["M_r"])
        P.tt("dve", M_i[:], S_i[:], cosM, ALU.mult, ["S_i", "cosT"], ["M_i"])
        P.tt("pool", G_i[:], S_r[:], sinM, ALU.mult, ["S_r", "sinT"], ["G_i"])
        P.tt("dve", M_i[:], M_i[:], G_i[:], ALU.subtract, ["M_i", "G_i"], ["M_i"])
        bounds = list(seq_starts) + [NCH]
        for j in range(16):
            for (Mx, Gx, key, gkey) in ((M_r, G_r, "M_r", "G_r"), (M_i, G_i, "M_i", "G_i")):
                for s in range(ns):
                    a, b = bounds[s], bounds[s + 1]
                    P.op("dve", lambda e, Mx=Mx, Gx=Gx, j=j, a=a, b=b: e.tensor_tensor_scan(
                        out=Gx[:, j, a:b], data0=rhot[:, j, a:b], data1=Mx[:, j, a:b], initial=0.0,
                        op0=ALU.mult, op1=ALU.add), [key, "rhot"], [gkey], 230.0)
        P.tt("dve", H_r[:], G_r[:], cosM, ALU.mult, ["G_r", "cosT"], ["H_r"])
        P.tt("pool", M_r[:], G_i[:], sinM, ALU.mult, ["G_i", "sinT"], ["M_r"])
        P.tt("dve", H_r[:], H_r[:], M_r[:], ALU.subtract, ["H_r", "M_r"], ["H_r"])
        P.tt("dve", H_i[:], G_i[:], cosM, ALU.mult, ["G_i", "cosT"], ["H_i"])
        P.tt("pool", M_i[:], G_r[:], sinM, ALU.mult, ["G_r", "sinT"], ["M_i"])
        P.tt("dve", H_i[:], H_i[:], M_i[:], ALU.add, ["H_i", "M_i"], ["H_i"])

    def ssm_end_state_only():
        s_compute()
        P.tt("dve", M_r[:], S_r[:], PvR_t[:], ALU.mult, ["S_r", "PvR"], ["M_r"])
        P.tt("pool", G_r[:], S_i[:], PvI_t[:], ALU.mult, ["S_i", "PvI"], ["G_r"])
        P.tt("dve", M_r[:], M_r[:], G_r[:], ALU.subtract, ["M_r", "G_r"], ["M_r"])
        P.tt("dve", M_i[:], S_i[:], PvR_t[:], ALU.mult, ["S_i", "PvR"], ["M_i"])
        P.tt("pool", G_i[:], S_r[:], PvI_t[:], ALU.mult, ["S_r", "PvI"], ["G_i"])
        P.tt("dve", M_i[:], M_i[:], G_i[:], ALU.add, ["M_i", "G_i"], ["M_i"])
        P.op("dve", lambda e: e.tensor_reduce(out=red_r[:], in_=M_r[:], op=ALU.add, axis=mybir.AxisListType.X),
             ["M_r"], ["red_r"], 700.0)
        P.op("dve", lambda e: e.tensor_reduce(out=red_i[:], in_=M_i[:], op=ALU.add, axis=mybir.AxisListType.X),
             ["M_i"], ["red_i"], 700.0)
        cmul_tab(inj_r[:, :, 0], inj_i[:, :, 0], aLr_t[:], aLi_t[:], Hc_r[:], Hc_i[:], it0[:, :, 0], it0[:, :, 1], "cy")
        P.tt("dve", Hc_r[:], inj_r[:, :, 0], red_r[:], ALU.add, ["cyr", "red_r"], ["Hin"])
        P.tt("dve", Hc_i[:], inj_i[:, :, 0], red_i[:], ALU.add, ["cyi", "red_i"], ["Hin"])

    def cmul_tab(dst_r, dst_i, tr_, ti_, hr, hi, tmp_r, tmp_i, kd_):
        P.tt("dve", dst_r, tr_, hr, ALU.mult, ["PcR", "aT", "Hin"], [kd_ + "r"])
        P.tt("pool", tmp_r, ti_, hi, ALU.mult, ["PcI", "aT", "Hin"], [kd_ + "tr"])
        P.tt("dve", dst_r, dst_r, tmp_r, ALU.subtract, [kd_ + "r", kd_ + "tr"], [kd_ + "r"])
        P.tt("dve", dst_i, tr_, hi, ALU.mult, ["PcR", "aT", "Hin"], [kd_ + "i"])
        P.tt("pool", tmp_i, ti_, hr, ALU.mult, ["PcI", "aT", "Hin"], [kd_ + "ti"])
        P.tt("dve", dst_i, dst_i, tmp_i, ALU.add, [kd_ + "i", kd_ + "ti"], [kd_ + "i"])

    def ssm_carry_prompt():
        P.cp("dve", Hc_r[:], H_r[:, :, NCH - 1], ["H_r", "Hprev", "ijr", "iji"], ["Hin"])
        P.cp("dve", Hc_i[:], H_i[:, :, NCH - 1], ["H_i", "Hprev", "ijr", "iji"], ["Hin"])

    def ssm_out(seq_starts, inj_fn):
        hr, hi = inj_fn()
        ns = len(seq_starts)
        step = NCH // ns
        P.cp("act", Hprev[:, :, 0, 1:NCH], H_r[:, :, 0:NCH - 1], ["H_r"], ["Hprev"])
        P.cp("pool", Hprev[:, :, 1, 1:NCH], H_i[:, :, 0:NCH - 1], ["H_i"], ["Hprev"])
        hr3 = hr if ns > 1 else hr.unsqueeze(2)
        hi3 = hi if ns > 1 else hi.unsqueeze(2)
        P.cp("dve", Hprev[:, :, 0, ::step], hr3, ["Hin", "Hprev"], ["Hprev"])
        P.cp("dve", Hprev[:, :, 1, ::step], hi3, ["Hin", "Hprev"], ["Hprev"])
        for ct in range(4):
            bank, bk = next_gb("Sh")
            yv = bank[:, 0:NST].rearrange("p (c r) -> p c r", r=T)
            P.mm(bank[:, 0:NST], Kblk[:, ct, 0, :], cur["uP"][:, ct, :, T - 1:2 * T - 1], True, False,
                 [("uT", cur["par"]), "Kblk"], bk)
            for r in range(T):
                for comp in range(2):
                    for jj in range(4):
                        j = 4 * ct + jj
                        P.mm(yv[jj * 32:(jj + 1) * 32, :, r], Cmod[:, j, r, comp, :], Hprev[:, j, comp, :],
                             False, False, ["Cmod", "Hprev"], bk, tp=(0, jj * 32))
            for l in range(1, T):
                P.mm(bank[:, 0:NST], Kblk[:, ct, l, :], cur["uP"][:, ct, :, T - 1 - l:2 * T - 1 - l], False, l == T - 1,
                     [("uT", cur["par"]), "Kblk"], bk)
            P.stt(y32[:].rearrange("p (c r) -> p c r", r=T), cur["uP"][:, ct, :, T - 1:2 * T - 1], dcol[:, ct:ct + 1], yv,
                  ALU.mult, ALU.add, [("uT", cur["par"]), "dcol"] + bk, ["y32"])
            P.act(yg[:, ct, :], y32[:], AF.Gelu, ["y32"], ["yg"])
        for ot in range(4):
            bank, bk = next_gb("Sh")
            for ct in range(4):
                P.mm(bank[:, 0:NST], w_glub[:, ct, ot * 128:(ot + 1) * 128], yg[:, ct, :], ct == 0, ct == 3,
                     ["yg", "wres"], bk)
            P.act(sg[:], bank[:, 0:NST], AF.Tanh, bk + ["bglu"], ["sg"], scale=0.5, bias=bglu[:, ot:ot + 1])
            P.stt(cur["catT"][:, 4 + ot, :], sg[:], 1.0, yg[:, ot, :], ALU.add, ALU.mult, ["yg", "sg"], ["catT_s"])

    def attn_chunk(cl, blocks):
        i = ctr["t"] % 2
        ctr["t"] += 1
        P.prio_bias = -1500
        pts = []
        for bi, (k_fn, v_ap, bias_ap, kkeys) in enumerate(blocks):
            bank, bk = next_gb("Q")
            for kv in range(2):
                for e2 in range(2):
                    v = 2 * kv + e2
                    P.mm(bank[:, v * 128:(v + 1) * 128], k_fn(v), cur["qT"][:, 2 * kv:2 * kv + 2, cl * 64:(cl + 1) * 64],
                         True, True, [("qT", cur["par"])] + kkeys, bk)
            pt = (PTa if bi == 0 else PTb)[i]
            pk = ("PT", bi, i)
            P.act(pt[:], bank[:], AF.Exp, bk + ["hbias", "zbias"], [pk], scale=0.125, bias=bias_ap)
            pts.append((pt, pk))
        stop_here("at1")
        obanks = [next_gb("M"), next_gb("M")]
        for hq in range(8):
            kv, g = hq // 4, hq % 4
            i2, e2 = g // 2, g % 2
            col = ((kv * 2 + e2) * 2 + i2) * 64
            bank, bk = obanks[hq // 4]
            o = bank[0:64, 0:260].rearrange("p (h c) -> p h c", c=65)[:, hq % 4, :]
            for bi, (k_fn, v_ap, bias_ap, kkeys) in enumerate(blocks):
                pt, pk = pts[bi]
                P.mm(o, pt[:, col:col + 64], v_ap[:, kv, :], bi == 0, bi == len(blocks) - 1,
                     [pk, "vc"] + kkeys, bk)
        stop_here("at2")
        for hb in range(2):
            bank, bk = obanks[hb]
            ov = bank[0:64, 0:260].rearrange("p (h c) -> p h c", c=65)
            P.tt("dve", den8[:, hb * 4:(hb + 1) * 4], ov[:, :, 64], sinkexp[0:64, hb * 4:(hb + 1) * 4], ALU.add,
                 bk + ["sinkexp"], ["den8"])
        P.op("dve", lambda e: e.reciprocal(out=den8[:], in_=den8[:]), ["den8"], ["den8"])
        for hb in range(2):
            bank, bk = obanks[hb]
            ov = bank[0:64, 0:260].rearrange("p (h c) -> p h c", c=65)
            P.tt("dve", Atok[:, hb * 4:(hb + 1) * 4, :], ov[:, :, 0:64],
                 den8[:, hb * 4:(hb + 1) * 4].unsqueeze(2).broadcast_to([64, 4, 64]), ALU.mult, bk + ["den8"], ["Atok"])
        stop_here("at3")
        tp, tk = next_tp("A")
        for t4 in range(4):
            P.tr(tp[:, t4 * 64:(t4 + 1) * 64], Atok[:, 2 * t4:2 * t4 + 2, :].rearrange("p a b -> p (a b)"),
                 identb[0:64, 0:64], ["Atok", "identb"], tk)
        P.cp("act", cur["catT"][:, 0:4, cl * 64:(cl + 1) * 64], tp[:, 0:256].rearrange("p (a b) -> p a b", b=64),
             tk, [("catT_a", cl)])
        P.prio_bias = 0

    def mix_ln1(tok0, m):
        banks = [next_gb("M"), next_gb("M")]
        for h in range(2):
            bank, bk = banks[h]
            for kt in range(KT):
                P.mm(bank[:], cur["catT"][:, kt, m * 128:(m + 1) * 128], w_outb[:, kt, h * 512:(h + 1) * 512], kt == 0,
                     kt == KT - 1, [("catT_a", 2 * m), ("catT_a", 2 * m + 1),
                                    "catT_s", "wres"], bk)
        i = ctr["x"] % 2
        ctr["x"] += 1
        z1 = z1s[i]
        zk = ("z1", i)
        for h in range(2):
            bank, bk = banks[h]
            P.tt("dve", z1[:, h * 512:(h + 1) * 512], bank[:], cur["r0"][m][:, h * 512:(h + 1) * 512], ALU.add,
                 bk + [("r0", m)], [zk])
        ln_tile(z1[:], z1, zk, None, None, 1)
        transpose_to(hTt[i], ("hTt", i), 0, 1)
        P.dma("act", "r1o%d" % i, lambda e, i=i, z1=z1: e.dma_start(out=r1_sc[tok0:tok0 + 128, :], in_=z1[:]),
              reads=[zk], final=True, nbytes=128 * D * 4)
        P.dma("act", "hTo%d" % i, lambda e, i=i: e.dma_start(out=hT_sc[:, :, tok0:tok0 + 128], in_=hTt[i][:]),
              reads=[("hTt", i)], final=True)

    for s in range(NMAIN // NST):
        last = (s == NMAIN // NST - 1)
        set_cur(s % 2)
        front(xprev[s * NST:(s + 1) * NST, :], last, False, halo_only=True)
        ssm_end_state_only()
    stop_here("c1")
    P.ts("dve", Hc_r[:], Hc_r[:], flag[:, 0:1], None, ALU.mult, None, ["Hin", "flag"], ["Hin"])
    P.ts("dve", Hc_i[:], Hc_i[:], flag[:, 0:1], None, ALU.mult, None, ["Hin", "flag"], ["Hin"])

    def kwin(w):
        return lambda v: kd[v][:, w * 128:(w + 1) * 128]

    def wkeys(w):
        return [("kd", w), ("vt", w)]

    for s in range(NMAIN // NST):
        set_cur(s % 2)
        front(xmain[s * NST:(s + 1) * NST, :], True, True)
        ssm_states([0], lambda: (Hc_r[:], Hc_i[:]))
        ssm_out([0], lambda: (Hc_r[:], Hc_i[:]))
        ssm_carry_prompt()
        if s == 0:
            stop_here("c2a")
        W = list(cur["w"])
        for cl in range(4):
            m = cl // 2
            halo = (s == 0 and m == 0)
            w0, w1 = W[m], W[m + 1]
            if cl % 2 == 0:
                blocks = [(kwin(w0), vt[:, w0, :, :], hbias[:, 0:1] if halo else zbias[:, 0:1], wkeys(w0)),
                          (kwin(w1), vt[:, w1, :, :], hbias[:, 3:4], wkeys(w1))]
            else:
                blocks = [(kwin(w0), vt[:, w0, :, :], hbias[:, 1:2] if halo else hbias[:, 2:3], wkeys(w0)),
                          (kwin(w1), vt[:, w1, :, :], zbias[:, 0:1], wkeys(w1))]
            attn_chunk(cl, blocks)
        if s == 0:
            stop_here("c2b")
        for m in range(2):
            mix_ln1(s * NST + m * 128, m)
        if s == 0:
            stop_here("c2")
        if s == NMAIN // NST - 1:
            bank, bk = next_gb("F")
            for kt in range(KT):
                P.mm(bank[:, 0:256], x0T[:, kt, 128:256], w_inb[:, kt, 512:768], kt == 0, kt == KT - 1,
                     ["x0T", "wres"], bk)
            P.tt("dve", kvo[:], bank[:, 0:256], ckvb[:], ALU.add, bk + ["ckvb"], ["kvo"])
            P.dma("sp", "kvo", lambda e: e.dma_start(out=kwin_p, in_=kvo[:, 0:128]), reads=["kvo"], final=True)
            P.dma("sp", "kvo", lambda e: e.dma_start(out=vwin_p, in_=kvo[:, 128:256]), reads=["kvo"], final=True)
            P.dma("sp", "sso", lambda e: e.dma_start(out=sre_p, in_=Hc_r[:]), reads=["Hin"], final=True)
            P.dma("sp", "sso", lambda e: e.dma_start(out=sim_p, in_=Hc_i[:]), reads=["Hin"], final=True)

    stop_here("c3")
    ld("c3", Hs_r[:], str_d, "Hin")
    ld("c3", Hs_i[:], sti_d, "Hin")
    for sq in range(4):
        P.dma("sp", "ck", lambda e, sq=sq: e.dma_start(out=ckf[:], in_=ck_d[sq]), writes=["ckf"])
        for kv in range(2):
            for e2 in range(2):
                v = 2 * kv + e2
                P.cp("dve", ckz[:, v, e2 * 64:(e2 + 1) * 64], ckf[:, kv * 64:(kv + 1) * 64], ["ckf", "ckz"], ["ckz"])
        tp, tk = next_tp("L")
        for v in range(4):
            P.tr(tp[:, v * 128:(v + 1) * 128], ckz[:, v, :], identb[:], ["ckz", "identb"], tk)
        for v in range(4):
            P.cp("act", kc[v][:, sq, :], tp[:, v * 128:(v + 1) * 128], tk, ["kc"])
        P.dma("sp", "wino", lambda e, sq=sq: e.dma_start(out=kwin_s[sq, 0:64, :], in_=ck_d[sq, 64:128, :]), final=True)
        P.dma("sp", "wino", lambda e, sq=sq: e.dma_start(out=vwin_s[sq, 0:64, :], in_=cv_d[sq, 64:128, :]), final=True)
        P.dma("sp", "ck", lambda e, sq=sq: e.dma_start(out=ckf[:], in_=cv_d[sq]), writes=["ckf"])
        P.cp("dve", vc[:, sq, :, 0:64], ckf[:].rearrange("p (a b) -> p a b", b=64), ["ckf"], ["vc"])
    set_cur(0)
    front(xsamp, True, True)
    W = list(cur["w"])
    starts = [0, 8, 16, 24]
    ssm_states(starts, lambda: (Hs_r[:], Hs_i[:]))
    ssm_out(starts, lambda: (Hs_r[:], Hs_i[:]))
    P.cp("dve", Ho_r[:], H_r[:, :, 7::8], ["H_r"], ["Ho"])
    P.cp("dve", Ho_i[:], H_i[:, :, 7::8], ["H_i"], ["Ho"])
    P.dma("sp", "sso", lambda e: e.dma_start(out=sre_s, in_=Ho_r[:]), reads=["Ho"], final=True)
    P.dma("sp", "sso", lambda e: e.dma_start(out=sim_s, in_=Ho_i[:]), reads=["Ho"], final=True)
    for cl in range(4):
        m = cl // 2
        w1 = W[m + 1]
        cache_blk = (lambda v, cl=cl: kc[v][:, cl, :], vc[:, cl, :, :], zbias[:, 0:1], ["kc"])
        own = (kwin(w1), vt[:, w1, :, :], hbias[:, 3:4] if cl % 2 == 0 else hbias[:, 2:3], wkeys(w1))
        attn_chunk(cl, [cache_blk, own])
    for m in range(2):
        mix_ln1(NMAIN + m * 128, m)
        bank, bk = next_gb("F")
        for kt in range(KT):
            P.mm(bank[:, 0:256], x0T[:, kt, m * 128:(m + 1) * 128], w_inb[:, kt, 512:768], kt == 0, kt == KT - 1,
                 ["x0T", "wres"], bk)
        P.tt("dve", kvo[:], bank[:, 0:256], ckvb[:], ALU.add, bk + ["ckvb"], ["kvo"])
        for q2 in range(2):
            sq = 2 * m + q2
            P.dma("sp", "kvo", lambda e, sq=sq, q2=q2: e.dma_start(out=kwin_s[sq, 64:128, :],
                                                                 in_=kvo[q2 * 64:(q2 + 1) * 64, 0:128]),
                  reads=["kvo"], final=True)
            P.dma("sp", "kvo", lambda e, sq=sq, q2=q2: e.dma_start(out=vwin_s[sq, 64:128, :],
                                                                 in_=kvo[q2 * 64:(q2 + 1) * 64, 128:256]),
                  reads=["kvo"], final=True)
    stop_here("c4")
    P.emit()
    nc.all_engine_barrier()
    ES.close()
    ES = ExitStack()

    Q = Prog(nc, "d_", ES)
    Q.pre_waits = list(DETACHED.values())
    gq = [Q.ps("gb%d" % i, [128, 512], F32) for i in range(8)]
    stq = dict(gb=0, w=0, d=0, t=0)

    def qnext():
        i = stq["gb"] % 8
        stq["gb"] += 1
        return gq[i], [("gb", i)]

    NH = NTOK // 2
    G2 = Q.sb("G2", [128, D], F32)
    B2 = Q.sb("B2", [128, D], F32)
    Q.dma("sp", "cg2", lambda e: e.dma_start(out=G2[:], in_=gbc_d[4]), writes=["G2"])
    Q.dma("sp", "cb2", lambda e: e.dma_start(out=B2[:], in_=gbc_d[5]), writes=["B2"])
    G1q = Q.sb("G1q", [128, D], F32)
    B1q = Q.sb("B1q", [128, D], F32)
    Q.dma("sp", "cg1", lambda e: e.dma_start(out=G1q[:], in_=gbc_d[2]), writes=["G1q"])
    Q.dma("sp", "cb1", lambda e: e.dma_start(out=B1q[:], in_=gbc_d[3]), writes=["B1q"])
    Q.op("act", lambda e: e.mul(out=G1q[:], in_=G1q[:], mul=ALPHA), ["G1q"], ["G1q"], 1100.0)
    Q.op("act", lambda e: e.mul(out=B1q[:], in_=B1q[:], mul=ALPHA), ["B1q"], ["B1q"], 1100.0)
    gcol2 = Q.sb("gcol2", [128, 6, KT], F32)
    Q.dma("sp", "cgc", lambda e: e.dma_start(out=gcol2[:], in_=gcol_d), writes=["gcol2"])
    hT = Q.sb("hT", [128, KT, NH], BF16)
    actT = Q.sb("actT", [128, NJ, NH], BF16)
    wdn = Q.sb("wdn", [128, NJ, D], BF16)
    wgu = [Q.sb("wgu%d" % i, [128, KT, 256], BF16) for i in range(3)]
    sgq = [Q.sb("sgq%d" % i, [128, 512], BF16) for i in range(2)]
    r1q = [Q.sb("r1q%d" % i, [128, D], F32) for i in range(2)]
    z2s = [Q.sb("z2_%d" % i, [128, D], F32) for i in range(2)]
    yo = [Q.sb("yo%d" % i, [128, D], F32) for i in range(2)]
    stats2 = Q.sb("stats2", [128, 2, 6], F32)
    mv2 = Q.sb("mv2", [128, 2], F32)
    rstd2 = Q.sb("rstd2", [128, 1], F32)
    nmr2 = Q.sb("nmr2", [128, 1], F32)
    mhalf2 = Q.sb("mhalf2", [128, 1], F32)
    Q.op("pool", lambda e: e.memset(mhalf2[:], -0.5), writes=["mhalf2"])
    for half in range(2):
        t0g = half * NH
        for kt in range(KT):
            Q.dma("sp", "hT%d" % kt, lambda e, kt=kt, t0g=t0g: e.dma_start(out=hT[:, kt, :], in_=hT_sc[:, kt, t0g:t0g + NH]),
                  writes=[("hT", kt)], nbytes=128 * NH * 2)
            Q.act(hT[:, kt, :], hT[:, kt, :], AF.Identity, [("hT", kt), "gcol2"], [("hT", kt)],
                  scale=gcol2[:, 2, kt:kt + 1], bias=gcol2[:, 3, kt:kt + 1])
        for j in range(NJ):
            wi = stq["w"] % 3
            stq["w"] += 1
            wnat = wgu_nat.rearrange("(k p) c -> p k c", p=128)
            for g in range(2):
                Q.dma("sp", "wgu%d_%d" % (wi, g), lambda e, wi=wi, j=j, g=g: e.dma_start(
                    out=wgu[wi][:, :, g * 128:(g + 1) * 128],
                    in_=wnat[:, :, g * DFF + j * 128:g * DFF + (j + 1) * 128]),
                    writes=[("wgu", wi, g)], nbytes=128 * KT * 128 * 2)
            if half == 0 and j == 2:
                for jq in range(0, NJ, 2):
                    Q.dma("sp", "wdn%d" % jq, lambda e, jq=jq: e.dma_start(
                        out=wdn[:, jq:jq + 2, :], in_=wdn_sc[jq * 128:(jq + 2) * 128, :].rearrange("(j p) c -> p j c", p=128)),
                        writes=[("wdn", jq), ("wdn", jq + 1)], nbytes=2 * 128 * D * 2)
            for (c0, cn) in ((0, 512), (512, 512), (1024, 128)):
                bg, bgk = qnext()
                bu, buk = qnext()
                for kt in range(KT):
                    Q.mm(bg[:, 0:cn], wgu[wi][:, kt, 0:128], hT[:, kt, c0:c0 + cn], kt == 0, kt == KT - 1,
                         [("wgu", wi, 0), ("hT", kt)], bgk)
                for kt in range(KT):
                    Q.mm(bu[:, 0:cn], wgu[wi][:, kt, 128:256], hT[:, kt, c0:c0 + cn], kt == 0, kt == KT - 1,
                         [("wgu", wi, 1), ("hT", kt)], buk)
                si = stq["d"] % 2
                stq["d"] += 1
                Q.act(sgq[si][:, 0:cn], bg[:, 0:cn], AF.Silu, bgk, [("sgq", si)])
                Q.tt("dve", actT[:, j, c0:c0 + cn], bu[:, 0:cn], sgq[si][:, 0:cn], ALU.mult, buk + [("sgq", si)],
                     [("actT", j, c0)])
        for tt_ in range(NH // 128):
            tok0 = t0g + tt_ * 128
            ri = stq["t"] % 2
            stq["t"] += 1
            Q.dma("sp", "r1q%d" % ri, lambda e, ri=ri, tok0=tok0: e.dma_start(out=r1q[ri][:], in_=r1_sc[tok0:tok0 + 128, :]),
                  writes=[("r1q", ri)])
            Q.tt("dve", r1q[ri][:], r1q[ri][:], G1q[:], ALU.mult, [("r1q", ri), "G1q"], [("r1q", ri)])
            Q.tt("pool", r1q[ri][:], r1q[ri][:], B1q[:], ALU.add, [("r1q", ri), "B1q"], [("r1q", ri)])
            banks = [qnext(), qnext()]
            for h in range(2):
                bank, bk = banks[h]
                for j in range(NJ):
                    Q.mm(bank[:], actT[:, j, tt_ * 128:(tt_ + 1) * 128], wdn[:, j, h * 512:(h + 1) * 512], j == 0,
                         j == NJ - 1, [("actT", j, (tt_ // 4) * 512), ("wdn", j)], bk)
            z2 = z2s[ri]
            zk = ("z2", ri)
            for h in range(2):
                bank, bk = banks[h]
                Q.tt("dve", z2[:, h * 512:(h + 1) * 512], bank[:], r1q[ri][:, h * 512:(h + 1) * 512], ALU.add,
                     bk + [("r1q", ri)], [zk])
            Q.op("dve", lambda e, z2=z2: e.bn_stats(out=stats2[:, 0, :], in_=z2[:, 0:512]), [zk], ["stats2"], 600.0)
            Q.op("dve", lambda e, z2=z2: e.bn_stats(out=stats2[:, 1, :], in_=z2[:, 512:1024]), [zk], ["stats2"], 600.0)
            Q.op("dve", lambda e: e.bn_aggr(out=mv2[:], in_=stats2[:].rearrange("p a b -> p (a b)")), ["stats2"], ["mv2"])
            Q.ts("dve", rstd2[:], mv2[:, 1:2], EPS, None, ALU.add, None, ["mv2"], ["rstd2"])
            Q.tt("pool", rstd2[:], rstd2[:], mhalf2[:], ALU.pow, ["rstd2", "mhalf2"], ["rstd2"])
            Q.stt(nmr2[:], mv2[:, 0:1], -1.0, rstd2[:], ALU.mult, ALU.mult, ["mv2", "rstd2"], ["nmr2"])
            Q.act(z2[:], z2[:], AF.Identity, [zk, "rstd2", "nmr2"], [zk], scale=rstd2[:, 0:1], bias=nmr2[:, 0:1])
            Q.tt("pool", yo[ri][:], z2[:], G2[:], ALU.mult, [zk, "G2"], [("yo", ri)])
            Q.tt("dve", yo[ri][:], yo[ri][:], B2[:], ALU.add, [("yo", ri), "B2"], [("yo", ri)])
            if tok0 < NMAIN:
                dst = y_main[tok0:tok0 + 128, :]
            else:
                dst = y_samp[tok0 - NMAIN:tok0 - NMAIN + 128, :]
            Q.dma("act", "yo%d" % ri, lambda e, ri=ri, dst=dst: e.dma_start(out=dst, in_=yo[ri][:]),
                  reads=[("yo", ri)], final=True)
    Q.emit()
    ES.close()


def make_in_maps(x_prompt, x_sample, cache_win_k, cache_win_v, state_ssm_re, state_ssm_im,
           ln_in_g, ln_in_b, w_in, attn_sinks, ssm_lambda_re, ssm_lambda_im, ssm_log_step,
           ssm_b_re, ssm_b_im, ssm_c_re, ssm_c_im, ssm_d, w_glu, b_glu, w_out,
           ln1_g, ln1_b, w_gate_up, w_down, ln2_g, ln2_b):
    f = lambda a: np.ascontiguousarray(np.asarray(a, dtype=np.float32))
    x_prompt, x_sample = f(x_prompt), f(x_sample)

    def st_layout(a):
        a = f(a)
        a = a.reshape((16, 2, 64) + a.shape[2:])
        a = np.moveaxis(a, 0, 2)
        return np.ascontiguousarray(a.reshape((128, 16) + a.shape[3:]))

    gvecs = [f(ln_in_g), f(ln_in_b), f(ln1_g)[0], f(ln1_b)[0], f(ln2_g)[0], f(ln2_b)[0]]
    gbc = np.ascontiguousarray(np.stack([np.broadcast_to(v[None, :], (128, D)) for v in gvecs]))
    gcol = np.ascontiguousarray(np.stack([v.reshape(KT, 128).T for v in gvecs], axis=1))
    shared = {
        "ident": np.eye(128, dtype=np.float32),
        "emask": np.ascontiguousarray((np.arange(128)[:, None] // 64 == np.arange(2)[None, :]).astype(np.float32)),
        "gbc": gbc, "gcol": gcol,
        "kvec": np.ascontiguousarray(np.broadcast_to(np.array(KLIST, np.float32)[None, :], (128, NK))),
        "lamr": st_layout(ssm_lambda_re[0]), "lami": st_layout(ssm_lambda_im[0]),
        "lstep": st_layout(np.broadcast_to(f(ssm_log_step)[0][:, None], (32, 64))),
        "bqr": st_layout(ssm_b_re[0]), "bqi": st_layout(ssm_b_im[0]),
        "cqr": st_layout(np.swapaxes(f(ssm_c_re)[0], 1, 2)), "cqi": st_layout(np.swapaxes(f(ssm_c_im)[0], 1, 2)),
        "dcol": np.ascontiguousarray(f(ssm_d)[0].reshape(4, 128).T),
        "bglucol": np.ascontiguousarray(f(b_glu)[0].reshape(4, 128).T),
        "sinkbc": np.ascontiguousarray(np.broadcast_to(f(attn_sinks)[0][None, :], (128, 8))),
        "w_in": f(w_in)[0], "w_glu": f(w_glu)[0], "w_out": f(w_out)[0],
        "w_gate_up": f(w_gate_up)[0], "w_down": f(w_down)[0],
    }
    in_maps = []
    for c in range(8):
        b, half = c // 2, c % 2
        m = dict(shared)
        m["xprev"] = np.ascontiguousarray(x_prompt[b, 0:NMAIN])
        m["xmain"] = np.ascontiguousarray(x_prompt[b, half * NMAIN:(half + 1) * NMAIN])
        m["xsamp"] = np.ascontiguousarray(x_sample[4 * c:4 * c + 4].reshape(NSAMP, D))
        m["flag"] = np.full((128, 1), float(half), np.float32)
        low = np.where(np.arange(128) < 64, -30000.0, 0.0)
        high = np.where(np.arange(128) >= 64, -30000.0, 0.0)
        halo = np.full(128, 0.0 if half else -30000.0)
        m["hbias"] = np.ascontiguousarray(np.stack([halo, np.minimum(halo, low), low, high], axis=1).astype(np.float32))
        m["ck"] = np.ascontiguousarray(f(cache_win_k)[0, 4 * c:4 * c + 4].reshape(4, 128, 128))
        m["cv"] = np.ascontiguousarray(f(cache_win_v)[0, 4 * c:4 * c + 4].reshape(4, 128, 128))
        m["st_re"] = np.ascontiguousarray(np.moveaxis(
            np.stack([st_layout(f(state_ssm_re)[0, 4 * c + s]) for s in range(4)]), 0, 2))
        m["st_im"] = np.ascontiguousarray(np.moveaxis(
            np.stack([st_layout(f(state_ssm_im)[0, 4 * c + s]) for s in range(4)]), 0, 2))
        in_maps.append(m)

    return in_maps


def assemble(R):

    def st_unlayout(a):
        a = np.asarray(a).reshape(2, 64, 16)
        return np.ascontiguousarray(np.transpose(a, (2, 0, 1)).reshape(32, 64))

    y_p = np.stack([np.concatenate([R[2 * b]["y_main"], R[2 * b + 1]["y_main"]], axis=0) for b in range(4)])
    y_s = np.concatenate([R[c]["y_samp"].reshape(4, 64, D) for c in range(8)], axis=0)
    kp = np.stack([R[2 * b + 1]["kwin_p"].reshape(128, 2, 64) for b in range(4)])[None]
    vp = np.stack([R[2 * b + 1]["vwin_p"].reshape(128, 2, 64) for b in range(4)])[None]
    rp = np.stack([st_unlayout(R[2 * b + 1]["sre_p"]) for b in range(4)])[None]
    ip = np.stack([st_unlayout(R[2 * b + 1]["sim_p"]) for b in range(4)])[None]
    ks = np.concatenate([R[c]["kwin_s"].reshape(4, 128, 2, 64) for c in range(8)], axis=0)[None]
    vs = np.concatenate([R[c]["vwin_s"].reshape(4, 128, 2, 64) for c in range(8)], axis=0)[None]
    rs = np.concatenate([np.stack([st_unlayout(R[c]["sre_s"][:, :, s]) for s in range(4)]) for c in range(8)])[None]
    is_ = np.concatenate([np.stack([st_unlayout(R[c]["sim_s"][:, :, s]) for s in range(4)]) for c in range(8)])[None]
    outs = (y_p, y_s, kp, vp, rp, ip, ks, vs, rs, is_)
    return tuple(np.ascontiguousarray(o.astype(np.float32)) for o in outs)


def kernel(**inputs):
    in_maps = make_in_maps(**inputs)
    nc = build_nc()
    res = run_bass_kernel_spmd(nc, in_maps, core_ids=list(range(8)))
    return assemble(res.results)
```
